# Optimizing a Trainium2 kernel written in Bass

```python
import jax, jax.numpy as jnp
from jax import lax
import numpy as np

D_MODEL = 1024
BATCH = 2
SEQ = 8192
DEPTH = 1

HEAD_DIM = 64
GRID_W = 64
NA_HEADS = 8
NA_KH = 8
NA_KW = 16
DIL_CONFIGS = ((128, 1), (512, 4), (2048, 16))
DIL_HEADS_PER_GROUP = 4
DIL_HEADS = DIL_HEADS_PER_GROUP * len(DIL_CONFIGS)
DIL_QBLOCK = 64
ROT_DIM = HEAD_DIM // 4
ROPE_THETA = 500000.0
D_FF = -(-8 * D_MODEL // (3 * 256)) * 256
EPS = 1e-6
NEG_INF = -1e30
WA = NA_HEADS * HEAD_DIM
WB = DIL_HEADS * HEAD_DIM
WB_OUT = DIL_HEADS_PER_GROUP * HEAD_DIM
W_IN = 3 * WA + 3 * WB + 2 * D_MODEL

kernel_name = "hybrid_natten_dilated_gated_encoder"

f32 = jnp.float32


def rms_norm(x, g):
    x32 = x.astype(f32)
    y = x32 * lax.rsqrt(jnp.mean(x32 * x32, axis=-1, keepdims=True) + EPS)
    return (y * g.astype(f32)).astype(x.dtype)


def partial_rotary(t, pos):
    half = ROT_DIM // 2
    inv_freq = ROPE_THETA ** (-(jnp.arange(half, dtype=f32) * 2.0) / ROT_DIM)
    ang = pos.astype(f32)[:, None] * inv_freq[None, :]
    cos = jnp.cos(ang)[None, :, None, :]
    sin = jnp.sin(ang)[None, :, None, :]
    x1 = t[..., :half].astype(f32)
    x2 = t[..., half:ROT_DIM].astype(f32)
    rot = jnp.concatenate([x1 * cos - x2 * sin, x2 * cos + x1 * sin], axis=-1).astype(t.dtype)
    return jnp.concatenate([rot, t[..., ROT_DIM:]], axis=-1)


def neighbourhood_attention(q, k, v, rpb):
    b, s, h, dh = q.shape
    rows = s // GRID_W
    kh = min(NA_KH, rows)
    qg = q.reshape(b, rows, GRID_W, h, dh)
    kg = k.reshape(b, rows, GRID_W, h, dh)
    vg = v.reshape(b, rows, GRID_W, h, dh)
    r = jnp.arange(rows)
    row_start = jnp.clip(r - kh // 2, 0, rows - kh)
    row_idx = row_start[:, None] + jnp.arange(kh)[None, :]
    kn = kg[:, row_idx].reshape(b, rows, kh * GRID_W, h, dh)
    vn = vg[:, row_idx].reshape(b, rows, kh * GRID_W, h, dh)
    col = jnp.arange(GRID_W)
    col_start = jnp.clip(col - NA_KW // 2, 0, GRID_W - NA_KW)
    col_mask = (col[None, :] >= col_start[:, None]) & (col[None, :] < col_start[:, None] + NA_KW)
    mask = jnp.broadcast_to(col_mask[:, None, :], (GRID_W, kh, GRID_W)).reshape(GRID_W, kh * GRID_W)
    row_off = row_idx - r[:, None] + (NA_KH - 1)
    col_off = jnp.clip(col[None, :] - col[:, None] + (NA_KW - 1), 0, 2 * NA_KW - 2)
    bias = rpb[:, row_off[:, None, :, None], col_off[None, :, None, :]]
    bias = bias.reshape(h, rows, GRID_W, kh * GRID_W).astype(f32)
    scores = jnp.einsum('brqhd,brkhd->bhrqk', qg, kn).astype(f32) * (dh ** -0.5) + bias[None]
    scores = jnp.where(mask, scores, NEG_INF)
    p = jax.nn.softmax(scores, axis=-1).astype(v.dtype)
    o = jnp.einsum('bhrqk,brkhd->brqhd', p, vn)
    return o.reshape(b, s, h * dh)


def dilated_window_attention(q, k, v, window, dilation):
    b, s, h, dh = q.shape
    half = (window // 2) // dilation
    seg = s // dilation

    def split(t):
        return t.reshape(b, seg, dilation, h, dh).transpose(0, 2, 1, 3, 4).reshape(b * dilation, seg, h, dh)

    qs, ks, vs = split(q), split(k), split(v)
    nb = -(-seg // DIL_QBLOCK)
    lp = nb * DIL_QBLOCK
    qs = jnp.pad(qs, ((0, 0), (0, lp - seg), (0, 0), (0, 0)))
    pad_k = ((0, 0), (half, lp - seg + half), (0, 0), (0, 0))
    kp, vp = jnp.pad(ks, pad_k), jnp.pad(vs, pad_k)
    span = DIL_QBLOCK + 2 * half
    key_idx = jnp.arange(nb)[:, None] * DIL_QBLOCK + jnp.arange(span)[None, :]
    kb = kp[:, key_idx]
    vb = vp[:, key_idx]
    qb = qs.reshape(-1, nb, DIL_QBLOCK, h, dh)
    qpos = jnp.arange(nb)[:, None] * DIL_QBLOCK + jnp.arange(DIL_QBLOCK)[None, :]
    kpos = key_idx - half
    rel = kpos[:, None, :] - qpos[:, :, None]
    mask = (jnp.abs(rel) <= half) & (kpos[:, None, :] >= 0) & (kpos[:, None, :] < seg)
    scores = jnp.einsum('nbqhd,nbkhd->nhbqk', qb, kb).astype(f32) * (dh ** -0.5)
    scores = jnp.where(mask[None, None], scores, NEG_INF)
    lse = jax.nn.logsumexp(scores, axis=-1)
    p = jnp.exp(scores - lse[..., None]).astype(v.dtype)
    o = jnp.einsum('nhbqk,nbkhd->nbqhd', p, vb).reshape(-1, lp, h, dh)[:, :seg]
    lse = lse.transpose(0, 2, 3, 1).reshape(-1, lp, h)[:, :seg]
    o = o.reshape(b, dilation, seg, h, dh).transpose(0, 2, 1, 3, 4).reshape(b, s, h, dh)
    lse = lse.reshape(b, dilation, seg, h).transpose(0, 2, 1, 3).reshape(b, s, h)
    return o, lse


def setup_inputs(seed: int = 0) -> dict:
    key = jax.random.key(seed)
    ks = jax.random.split(key, 20)
    nrm = lambda k, shape, scale: jax.random.normal(k, shape, f32) * scale
    d = D_MODEL
    return {
        "x": nrm(ks[0], (BATCH, SEQ, d), 1.0),
        "c": nrm(ks[1], (BATCH, d), 1.0),
        "w_ada": nrm(ks[2], (DEPTH, d, 6 * d), 0.5 * d ** -0.5),
        "b_ada": nrm(ks[3], (DEPTH, 6 * d), 0.02),
        "g_norm1": 1.0 + nrm(ks[4], (DEPTH, d), 0.05),
        "g_norm2": 1.0 + nrm(ks[5], (DEPTH, d), 0.05),
        "w_in": nrm(ks[6], (DEPTH, d, W_IN), d ** -0.5),
        "b_gate": nrm(ks[7], (DEPTH, 2 * d), 0.02),
        "g_qa": 1.0 + nrm(ks[8], (DEPTH, HEAD_DIM), 0.05),
        "g_ka": 1.0 + nrm(ks[9], (DEPTH, HEAD_DIM), 0.05),
        "g_qb": 1.0 + nrm(ks[10], (DEPTH, HEAD_DIM), 0.05),
        "g_kb": 1.0 + nrm(ks[11], (DEPTH, HEAD_DIM), 0.05),
        "rpb": nrm(ks[12], (DEPTH, NA_HEADS, 2 * NA_KH - 1, 2 * NA_KW - 1), 0.1),
        "w_proj_a": nrm(ks[13], (DEPTH, WA, d), WA ** -0.5),
        "w_proj_b": nrm(ks[14], (DEPTH, WB_OUT, d), WB_OUT ** -0.5),
        "w_o": nrm(ks[15], (DEPTH, d, d), d ** -0.5),
        "w_ffn_in": nrm(ks[16], (DEPTH, d, 2 * D_FF), d ** -0.5),
        "w_ffn_out": nrm(ks[17], (DEPTH, D_FF, d), D_FF ** -0.5),
    }


def reference(x, c, w_ada, b_ada, g_norm1, g_norm2, w_in, b_gate, g_qa, g_ka, g_qb, g_kb,
              rpb, w_proj_a, w_proj_b, w_o, w_ffn_in, w_ffn_out):
    b, s, _ = x.shape
    pos = jnp.arange(s)
    c_act = jax.nn.silu(c)
    split_at = [WA, 2 * WA, 3 * WA, 3 * WA + WB, 3 * WA + 2 * WB, 3 * WA + 3 * WB]
    for l in range(DEPTH):
        mod = c_act @ w_ada[l] + b_ada[l]
        sh1, sc1, gt1, sh2, sc2, gt2 = [m[:, None, :] for m in jnp.split(mod, 6, axis=-1)]

        h = rms_norm(x, g_norm1[l]) * (1.0 + sc1) + sh1
        proj = h @ w_in[l]
        qa, ka, va, qb, kb, vb, gates = jnp.split(proj, split_at, axis=-1)

        qa = rms_norm(qa.reshape(b, s, NA_HEADS, HEAD_DIM), g_qa[l])
        ka = rms_norm(ka.reshape(b, s, NA_HEADS, HEAD_DIM), g_ka[l])
        va = va.reshape(b, s, NA_HEADS, HEAD_DIM)
        o_a = neighbourhood_attention(qa, ka, va, rpb[l])

        qb = partial_rotary(rms_norm(qb.reshape(b, s, DIL_HEADS, HEAD_DIM), g_qb[l]), pos)
        kb = partial_rotary(rms_norm(kb.reshape(b, s, DIL_HEADS, HEAD_DIM), g_kb[l]), pos)
        vb = vb.reshape(b, s, DIL_HEADS, HEAD_DIM)
        outs, lses = [], []
        for g, (win, dil) in enumerate(DIL_CONFIGS):
            sl = slice(g * DIL_HEADS_PER_GROUP, (g + 1) * DIL_HEADS_PER_GROUP)
            o_g, lse_g = dilated_window_attention(qb[:, :, sl], kb[:, :, sl], vb[:, :, sl], win, dil)
            outs.append(o_g)
            lses.append(lse_g)
        wts = jax.nn.softmax(jnp.stack(lses, axis=0), axis=0)
        o_b = jnp.einsum('gbsh,gbshd->bshd', wts.astype(vb.dtype), jnp.stack(outs, axis=0))
        o_b = o_b.reshape(b, s, WB_OUT)

        gate_a, gate_b = jnp.split(jax.nn.sigmoid(gates + b_gate[l]), 2, axis=-1)
        merged = gate_a * (o_a @ w_proj_a[l]) + gate_b * (o_b @ w_proj_b[l])
        x = x + gt1 * (merged @ w_o[l])

        h2 = rms_norm(x, g_norm2[l]) * (1.0 + sc2) + sh2
        a, up = jnp.split(h2 @ w_ffn_in[l], 2, axis=-1)
        x = x + gt2 * ((jax.nn.silu(a) * up) @ w_ffn_out[l])
    return x
```

```python
import contextlib
import numpy as np
import concourse.bass as bass
import concourse.mybir as mybir
from concourse.bass_utils import run_bass_kernel_spmd

F32 = mybir.dt.float32
BF16 = mybir.dt.bfloat16
AF = mybir.ActivationFunctionType
ALU = mybir.AluOpType
NEG = -30000.0
EPS = 1e-6
KB = 1024
DIL = (1, 4, 16)


class _Op:
    __slots__ = ("eng", "fn", "idx", "deps", "signal", "count", "dma_sem", "dma_val", "waits")

    def __init__(self, eng, fn, idx):
        self.eng = eng
        self.fn = fn
        self.idx = idx
        self.deps = set()
        self.signal = False
        self.count = 0
        self.dma_sem = None
        self.dma_val = 0
        self.waits = []


class Prog:
    ENGS = ("pe", "act", "dve", "pool", "sp")

    def __init__(self):
        self.ops = {e: [] for e in self.ENGS}
        self.writers = {}
        self.readers = {}
        self.dma_keys = {}
        self.last_dma = {}
        self.bar = set()

    def _add(self, eng, fn, reads, writes, semkey=None):
        o = _Op(eng, fn, len(self.ops[eng]))
        deps = set(self.bar)
        for r in reads:
            deps.update(self.writers.get(r, {}).values())
        for k in writes:
            deps.update(self.writers.get(k, {}).values())
            deps.update(self.readers.get(k, ()))
        deps.discard(o)
        o.deps = deps
        for r in reads:
            self.readers.setdefault(r, set()).add(o)
        wk = eng if semkey is None else ("dma", semkey)
        for k in writes:
            self.writers.setdefault(k, {})[wk] = o
            self.readers[k] = set()
        self.ops[eng].append(o)
        return o

    def op(self, eng, fn, reads=(), writes=()):
        return self._add(eng, fn, tuple(reads), tuple(writes))

    def dma(self, eng, fn, semkey, reads=(), writes=()):
        o = self._add(eng, fn, tuple(reads), tuple(writes), semkey)
        n = self.dma_keys.get(semkey, 0) + 1
        self.dma_keys[semkey] = n
        o.dma_sem = semkey
        o.dma_val = 16 * n
        self.last_dma[semkey] = o
        return o

    def alias(self, new_keys, old_keys):
        ws, rs = {}, set()
        for k in old_keys:
            for wk, o in self.writers.get(k, {}).items():
                cur = ws.get(wk)
                if cur is None or (o.dma_val if o.dma_sem is not None else o.idx) > \
                        (cur.dma_val if cur.dma_sem is not None else cur.idx):
                    ws[wk] = o
            rs |= self.readers.get(k, set())
        for nk in new_keys:
            d = self.writers.setdefault(nk, {})
            for wk, o in ws.items():
                cur = d.get(wk)
                if cur is None or (o.dma_val if o.dma_sem is not None else o.idx) > \
                        (cur.dma_val if cur.dma_sem is not None else cur.idx):
                    d[wk] = o
            self.readers.setdefault(nk, set()).update(rs)

    def barrier(self):
        b = set()
        for e in self.ENGS:
            for o in reversed(self.ops[e]):
                if o.dma_sem is None:
                    b.add(o)
                    break
        b.update(self.last_dma.values())
        self.bar = b
        self.writers = {}
        self.readers = {}

    def resolve(self):
        for X in self.ENGS:
            waited = {e: -1 for e in self.ENGS}
            waited_sem = {}
            for o in self.ops[X]:
                best = {}
                for d in o.deps:
                    if d.dma_sem is not None:
                        if waited_sem.get(d.dma_sem, 0) < d.dma_val:
                            k = ("s", d.dma_sem)
                            if k not in best or d.dma_val > best[k].dma_val:
                                best[k] = d
                    else:
                        if d.eng == X and X == "pe" and o.dma_sem is None:
                            continue
                        if d.idx > waited[d.eng]:
                            k = ("e", d.eng)
                            if k not in best or d.idx > best[k].idx:
                                best[k] = d
                for k, d in best.items():
                    if k[0] == "s":
                        waited_sem[d.dma_sem] = d.dma_val
                    else:
                        waited[d.eng] = d.idx
                        d.signal = True
                o.waits = list(best.values())
        for E in self.ENGS:
            c = 0
            for o in self.ops[E]:
                if o.dma_sem is None and o.signal:
                    c += 1
                    o.count = c

    def emit(self, block, sems, dsems):
        self.resolve()
        handles = {"pe": "tensor", "act": "scalar", "dve": "vector", "pool": "gpsimd", "sp": "sync"}
        prog = self

        def make(E):
            def body(eng):
                for o in prog.ops[E]:
                    for d in o.waits:
                        if d.dma_sem is not None:
                            eng.wait_ge(dsems[d.dma_sem], d.dma_val)
                        else:
                            eng.wait_ge(sems[d.eng], d.count)
                    ins = o.fn(eng)
                    if o.dma_sem is not None:
                        ins.then_inc(dsems[o.dma_sem], 16)
                    elif o.signal:
                        ins.then_inc(sems[E], 1)
                if E == "sp":
                    for k, n in prog.dma_keys.items():
                        eng.wait_ge(dsems[k], 16 * n)
            return body

        for E in self.ENGS:
            getattr(block, handles[E])(make(E))


def _tile_w(w):
    K, N = w.shape
    return np.ascontiguousarray(w.reshape(K // 128, 128, N // 128, 128).transpose(2, 1, 0, 3))


def _col_vec(v):
    return np.ascontiguousarray(v.reshape(-1, 128).T)


def _a_tiles(b):
    if b == 0:
        return list(range(0, 6))
    if b == 15:
        return list(range(14, 20))
    return list(range(b, b + 5))


EDGE_BLOCKS = (0, 1, 14, 15)
NV = 181


def _constants():
    c = np.zeros((128, 9 * 128), np.float32)
    idx = np.arange(128)
    c[:, 0:128] = np.eye(128, dtype=np.float32)
    c[:, 128:256] = 1.0
    c[:, 256:384] = (idx[:, None] // 64 == idx[None, :] // 64).astype(np.float32)
    r = np.zeros((128, 128), np.float32)
    for m in range(128):
        dd = m % 64
        if dd < 8:
            r[m + 8, m] = 1.0
        elif dd < 16:
            r[m - 8, m] = 1.0
    c[:, 384:512] = r
    k = idx[:, None]
    q = idx[None, :]
    mk0 = np.where(k >= q, 0.0, NEG)
    mk1 = np.where(k <= q, 0.0, NEG)
    c[:, 512:640] = mk0
    c[:, 640:768] = mk0
    c[:, 768:896] = mk1
    c[:, 896:1024] = mk1
    c[0, 1024:1088] = 1.0
    c[1, 1088:1152] = 1.0
    return c


def _strips(rpb):
    qc = np.arange(64)[None, :]
    kc = np.arange(64)[:, None]
    cs = np.clip(qc - 8, 0, 48)
    colmask = (kc >= cs) & (kc < cs + 16)
    coff = np.clip(kc - qc + 15, 0, 30)
    out = np.full((8, 2, 128, 16 * 64), NEG, np.float32)
    for ty in range(2):
        for half in range(2):
            for i in range(16):
                dr = 7 - i + half
                if abs(dr) > 7:
                    continue
                if ty == 0 and not (-4 <= dr <= 3):
                    continue
                g = rpb[:, dr + 7, :][:, coff]
                blk = np.where(colmask[None], g, np.float32(NEG))
                out[:, ty, half * 64:(half + 1) * 64, i * 64:(i + 1) * 64] = blk
    return out


def _rowmask(q):
    rm = np.full((2, 12 * 256), NEG, np.float32)
    for ge, G in enumerate((0, 7)):
        for s_ in range(6):
            t = 2 * G + s_
            ix = ge * 6 + s_
            for kl in range(2):
                kr = 32 * q + 2 * t - 4 + kl
                for j in range(4):
                    qr = 32 * q + 4 * G + j
                    rs = min(max(qr - 4, 0), 120)
                    if rs <= kr < rs + 8:
                        rm[kl, ix * 256 + j * 64: ix * 256 + (j + 1) * 64] = 0.0
    return rm


def _rope(q):
    pos = (2048 * q - 1024 + np.arange(4096)).astype(np.float32)
    inv = (np.float32(500000.0) ** (-(np.arange(8, dtype=np.float32) * np.float32(2.0)) / np.float32(16))).astype(np.float32)
    ang = (pos[:, None] * inv[None, :]).astype(np.float32).astype(np.float64)
    cs = np.cos(ang).astype(np.float32).T
    sn = np.sin(ang).astype(np.float32).T
    C = np.ones((128, 4096), np.float32)
    S = np.zeros((128, 4096), np.float32)
    for p in range(128):
        dd = p % 64
        if dd < 8:
            C[p] = cs[dd]
            S[p] = -sn[dd]
        elif dd < 16:
            C[p] = cs[dd - 8]
            S[p] = sn[dd - 8]
    return C, S


def _vmask(q):
    vm = np.ones((128, 89), np.float32)
    col = 20
    p = np.arange(128)
    for g, d in enumerate(DIL):
        lx = 2048 // d + 128
        nt = lx // 128
        for r in range(d):
            for i in range(nt):
                tok = 2048 * q - 64 * d + d * (128 * i + p) + r
                vm[:, col + r * nt + i] = ((tok >= 0) & (tok < 8192)).astype(np.float32)
        col += d * nt
    return vm


def build_program(debug=False):
    nc = bass.Bass("TRN2", target_bir_lowering=False)

    def din(name, shape):
        return nc.dram_tensor(name, shape, F32, kind="ExternalInput").ap()

    xT = din("xT", [8, 128, 8, 512])
    vecs_d = din("vecs", [128, NV])
    cst_d = din("cst", [128, 1152])
    wada_d = din("wada", [48, 128, 8, 128])
    win_d = din("win", [46, 128, 8, 128])
    wpa_d = din("wpa", [8, 128, 4, 128])
    wpb_d = din("wpb", [8, 128, 2, 128])
    wo_d = din("wo", [8, 128, 8, 128])
    wfi_d = din("wfi", [44, 128, 8, 128])
    wfo_d = din("wfo", [22, 128, 1024])
    strips_d = din("strips", [8, 2, 128, 1024])
    rm_d = din("rm", [2, 12 * 256])
    ropeC_d = din("ropeC", [128, 4096])
    ropeS_d = din("ropeS", [128, 4096])
    outT = nc.dram_tensor("outT", [128, 8, 2048], F32, kind="ExternalOutput").ap()
    dbg = {}
    if debug:
        dbg["hT"] = nc.dram_tensor("dbg_hT", [128, 8, 4096], F32, kind="ExternalOutput").ap()
        dbg["oaT"] = nc.dram_tensor("dbg_oaT", [128, 4, 2048], F32, kind="ExternalOutput").ap()
        dbg["obT"] = nc.dram_tensor("dbg_obT", [128, 2, 2048], F32, kind="ExternalOutput").ap()
        dbg["x1"] = nc.dram_tensor("dbg_x1", [128, 8, 2048], F32, kind="ExternalOutput").ap()
        dbg["mod"] = nc.dram_tensor("dbg_mod", [128, 48], F32, kind="ExternalOutput").ap()

    P = Prog()
    ARENA_KB = 192
    with contextlib.ExitStack() as es:
        arena = es.enter_context(nc.sbuf_tensor("arena", [128, ARENA_KB * KB // 2], BF16))
        cstb = es.enter_context(nc.sbuf_tensor("cstb", [128, 1152], BF16))
        vecs = es.enter_context(nc.sbuf_tensor("vecs_sb", [128, NV], F32))
        modT = es.enter_context(nc.sbuf_tensor("modT", [128, 48], F32))
        small = es.enter_context(nc.sbuf_tensor("small", [128, 64], F32))
        cactb_t = es.enter_context(nc.sbuf_tensor("cactb", [128, 16], BF16))
        pst = [es.enter_context(nc.psum_tensor(f"ps{i}", [128, 512], F32)) for i in range(8)]

        def av(off_b, nbytes, dt):
            a = arena[:, off_b // 2:(off_b + nbytes) // 2]
            return a if dt == BF16 else a.bitcast(F32)

        ident = cstb[:, 0:128]
        ones_b = cstb[:, 128:256]
        bones = cstb[:, 256:384]
        rmat = cstb[:, 384:512]
        mk2 = [cstb[:, 512:768], cstb[:, 768:1024]]
        oh2 = cstb[0:2, 1024:1152]

        V_C, V_BADA, V_G1, V_G2, V_BG, V_GQK, V_VM = 0, 8, 56, 64, 72, 88, 92
        gsc1 = small[:, 0:8]
        gsc2 = small[:, 8:16]
        gq = small[:, 16:20]
        cact2 = small[:, 24:40].rearrange("p (k n) -> p k n", n=2)
        stmp = small[:, 40:48]
        stmp2 = small[:, 48:56]

        bank_ctr = [0]
        bank_pool = [0, 1, 2, 3, 4, 5, 6]

        def nextbank():
            i = bank_pool[bank_ctr[0] % len(bank_pool)]
            bank_ctr[0] += 1
            return pst[i], f"ps{i}"

        def pipeline(items, stages):
            n, S = len(items), len(stages)
            st = [dict() for _ in items]
            for t in range(n + S - 1):
                for sidx in range(S):
                    i = t - sidx
                    if 0 <= i < n:
                        stages[sidx](items[i], st[i])

        def MM(out, lhsT, rhs, start, stop, r, w):
            P.op("pe", lambda e: e.matmul(out, lhsT=lhsT, rhs=rhs, start=start, stop=stop), r, w)

        def ACTF(out, in_, func, r, w, **kw):
            P.op("act", lambda e: e.activation(out=out, in_=in_, func=func, **kw), r, w)

        def TT(eng, out, in0, in1, op, r, w):
            P.op(eng, lambda e: e.tensor_tensor(out=out, in0=in0, in1=in1, op=op), r, w)

        def STT(out, in0, scalar, in1, op0, op1, r, w):
            P.op("dve", lambda e: e.scalar_tensor_tensor(out=out, in0=in0, scalar=scalar, in1=in1, op0=op0, op1=op1), r, w)

        def TS(eng, out, in0, s1, op0, r, w, s2=None, op1=None):
            if s2 is None:
                P.op(eng, lambda e: e.tensor_scalar(out=out, in0=in0, scalar1=s1, scalar2=None, op0=op0), r, w)
            else:
                P.op(eng, lambda e: e.tensor_scalar(out=out, in0=in0, scalar1=s1, scalar2=s2, op0=op0, op1=op1), r, w)

        def DMA(q, out, in_, key, r, w):
            P.dma(q, lambda e: e.dma_start(out=out, in_=in_), key, r, w)

        DMA("pool", cstb[:, :], cst_d[:, :], "cst", [], ["cst"])
        DMA("sp", vecs[:, :], vecs_d[:, :], "vecs", [], ["vecs"])
        cT = vecs[:, V_C:V_C + 8]
        ACTF(stmp, cT, AF.Exp, ["vecs"], ["stmp"], scale=-1.0)
        TS("dve", stmp, stmp, 1.0, ALU.add, ["stmp"], ["stmp"])
        P.op("dve", lambda e: e.reciprocal(out=stmp, in_=stmp), ["stmp"], ["stmp"])
        TT("dve", cact2[:, :, 0], cT, stmp, ALU.mult, ["vecs", "stmp"], ["cact"])
        TT("dve", cact2[:, :, 1], cT, stmp, ALU.mult, ["vecs", "stmp"], ["cact"])
        S0 = 88 * KB
        wada_ring = [av(64 * KB + i * 2 * KB, 2 * KB, BF16).rearrange("p (k c) -> p k c", k=8) for i in range(8)]
        cactb = cactb_t[:, :].rearrange("p (k n) -> p k n", n=2)
        P.op("dve", lambda e: e.tensor_copy(out=cactb, in_=cact2), ["cact"], ["cactb"])
        mbank, mkey = pst[7], "ps7"

        def mod_dma(ring, j0, j1):
            for j in range(j0, j1):
                DMA("pool", ring[j % 8], wada_d[j], f"wa{j % 8}", [], [f"wa{j % 8}"])

        def mod_mm(ring, bk, bkey, j0, j1):
            for j in range(j0, j1):
                for kc in range(8):
                    MM(bk[:, 2 * (j - j0):2 * (j - j0) + 2], ring[j % 8][:, kc, :], cactb[:, kc, :], kc == 0, kc == 7,
                       [f"wa{j % 8}", "cactb"], [bkey])
            mv = bk[:, 0:2 * (j1 - j0)].rearrange("p (j n) -> p j n", n=2)[:, :, 0]
            TT("dve", modT[:, j0:j1], mv, vecs[:, V_BADA + j0:V_BADA + j1], ALU.add, [bkey, "vecs"], ["modT"])

        mod_dma(wada_ring, 0, 8)
        mod_mm(wada_ring, mbank, mkey, 0, 8)
        mod_dma(wada_ring, 8, 16)
        mod_mm(wada_ring, mbank, mkey, 8, 16)
        TS("dve", stmp, modT[:, 8:16], 1.0, ALU.add, ["modT"], ["stmp"])
        TT("dve", gsc1, stmp, vecs[:, V_G1:V_G1 + 8], ALU.mult, ["stmp", "vecs"], ["gsc"])
        TS("dve", gq[:, :], vecs[:, V_GQK:V_GQK + 4], 1.0, ALU.mult, ["vecs"], ["gq"])
        TS("dve", gq[:, 0:1], vecs[:, V_GQK:V_GQK + 1], 0.125, ALU.mult, ["vecs", "gq"], ["gq"])
        TS("dve", gq[:, 2:3], vecs[:, V_GQK + 2:V_GQK + 3], 0.125, ALU.mult, ["vecs", "gq"], ["gq"])

        hT = av(0, 64 * KB, BF16).rearrange("p (k e) -> p k e", k=8)

        def hkeys(e0, n):
            return [f"hT{c}" for c in range(e0 // 512, (e0 + n - 1) // 512 + 1)]

        nrm_ctr = [0]

        def nrm_s0(it, st):
            i = nrm_ctr[0] % 2
            nrm_ctr[0] += 1
            st["i"] = i
            if it.get("pre") is not None:
                it["pre"]()
            sqb_ = it["bufs"]["sqb"][i]
            ACTF(sqb_, it["xb"], AF.Square, [it["xkey"]], [f"sqb{i}"])
            bk, bkey = it.get("bankfn", nextbank)()
            for kc in range(8):
                MM(bk[:, :], ones_b, sqb_[:, kc, :], kc == 0, kc == 7, [f"sqb{i}", "cst"], [bkey])
            st["bk"], st["bkey"] = bk, bkey

        def nrm_s1(it, st):
            i = st["i"]
            lnf_, tb = it["bufs"]["lnf"][i], it["bufs"]["tmpb"][i]
            tkey = it["bufs"].get("tkeys", ["tmpb0", "tmpb1"])[i]
            ACTF(lnf_, st["bk"][:, :], AF.Ln, [st["bkey"]], [f"lnf{i}"], scale=1.0 / 1024.0, bias=EPS)
            rsb, rskey = it.get("bankfn", nextbank)()
            ACTF(rsb[:, :], lnf_, AF.Exp, [f"lnf{i}"], [rskey], scale=-0.5)
            TT("dve", tb, it["xb"], rsb[:, :].unsqueeze(1).to_broadcast([128, 8, 512]), ALU.mult,
               [it["xkey"], rskey], [tkey])

        def nrm_s2(it, st):
            i = st["i"]
            tb = it["bufs"]["tmpb"][i]
            tkey = it["bufs"].get("tkeys", ["tmpb0", "tmpb1"])[i]
            gsc, shcol = it["gsc"], it["shcol"]
            for kc in range(8):
                sc_ap = gsc[:, kc:kc + 1]
                sh_ap = modT[:, shcol + kc:shcol + kc + 1]
                if kc % 2 == 0:
                    ACTF(it["dst_fn"](kc), tb[:, kc, :], AF.Identity, [tkey, "modT", "gsc"], it["dkeys"],
                         scale=sc_ap, bias=sh_ap)
                else:
                    TS("pool", it["dst_fn"](kc), tb[:, kc, :], sc_ap, ALU.mult, [tkey, "modT", "gsc"], it["dkeys"],
                       s2=sh_ap, op1=ALU.add)

        xbuf = [av(S0 + i * 16 * KB, 16 * KB, F32).rearrange("p (k n) -> p k n", k=8) for i in range(3)]
        nb1 = dict(sqb=[av(S0 + 48 * KB + i * 8 * KB, 8 * KB, BF16).rearrange("p (k n) -> p k n", k=8) for i in range(2)],
                   tmpb=[av(S0 + 64 * KB + i * 16 * KB, 16 * KB, F32).rearrange("p (k n) -> p k n", k=8) for i in range(2)],
                   lnf=[av(S0 + 96 * KB + i * 2 * KB, 2 * KB, F32) for i in range(2)])
        items = []
        for ec in range(8):
            def pre(ec=ec):
                DMA("sp", xbuf[ec % 3], xT[ec], f"xb{ec % 3}", [], [f"xb{ec % 3}"])
            items.append(dict(xb=xbuf[ec % 3], xkey=f"xb{ec % 3}", gsc=gsc1, shcol=0, pre=pre, bufs=nb1,
                              dst_fn=lambda kc, ec=ec: hT[:, kc, ec * 512:(ec + 1) * 512], dkeys=[f"hT{ec}"]))
        pipeline(items, [nrm_s0, nrm_s1, nrm_s2])
        bank_pool.append(7)
        if debug:
            dh = av(S0 + 64 * KB, 16 * KB, F32).rearrange("p (k n) -> p k n", k=8)
            for ec in range(8):
                P.op("dve", lambda e, ec=ec, dh=dh: e.tensor_copy(out=dh, in_=hT[:, :, ec * 512:(ec + 1) * 512]), [f"hT{ec}"], ["dh", "tmpb0"])
                DMA("sp", dbg["hT"][:, :, ec * 512:(ec + 1) * 512], dh, "dbg", ["dh"], [])
            P.barrier()
        P.alias(["oaT", "obT"], [f"wa{i}" for i in range(8)])
        P.alias(["jq", "jk"], ["xb0"])
        P.alias(["jv"], ["xb1"])
        P.alias(["stk"], ["xb1", "xb2"])
        P.alias(["rmb", "wr0", "wr1"], ["xb2"])
        P.alias(["wr2", "wr3", "wr4", "wr5"], ["sqb0"])
        P.alias(["pt0", "pt1"], ["sqb1", "tmpb0"])
        P.alias(["qsq0", "qsq1", "qln0", "qln1", "qrs0", "qrs1", "qqn0", "qqn1", "qqn2"], ["tmpb0", "tmpb1"])
        P.alias([f"wa{i}" for i in range(8)], ["tmpb1", "lnf0"])
        P.alias(["rcb0", "rcb1", "onesf"], ["lnf0", "lnf1"])

        oaT = av(64 * KB, 16 * KB, BF16).rearrange("p (k n) -> p k n", k=4)
        obT = av(80 * KB, 8 * KB, BF16).rearrange("p (k n) -> p k n", k=2)
        o = S0
        jq2 = av(o, 8 * KB, BF16).rearrange("p (h n) -> p h n", h=2); o += 8 * KB
        jk = av(o, 8 * KB, BF16); o += 8 * KB
        jv = av(o, 12 * KB, BF16).rearrange("p (t c) -> p t c", c=192); o += 12 * KB
        acc = [av(o + i * 8 * KB, 8 * KB, F32) for i in range(2)]
        stripk = av(o, 8 * KB, BF16).rearrange("p (h t c) -> p h t c", h=2, t=2)
        rmb = av(o + 8 * KB, 6 * KB, BF16)
        o += 16 * KB
        wring = [av(o + i * 2 * KB, 2 * KB, BF16).rearrange("p (k c) -> p k c", k=8) for i in range(6)]; o += 12 * KB
        ptb = [av(o + i * 6 * KB, 6 * KB, BF16) for i in range(2)]; o += 12 * KB
        q_sq = [av(o + i * KB, KB, BF16) for i in range(2)]; o += 2 * KB
        q_ln = [av(o + i * 2 * KB, 2 * KB, F32) for i in range(2)]; o += 4 * KB
        q_rs = [av(o + i * 2 * KB, 2 * KB, F32) for i in range(2)]; o += 4 * KB
        q_qn = [av(o + i * KB, KB, BF16) for i in range(3)]; o += 3 * KB
        q_t1_off = o
        q_t1 = [av(o + i * 2 * KB, 2 * KB, F32) for i in range(2)]; o += 4 * KB
        q_t2 = [av(o + i * 2 * KB, 2 * KB, F32) for i in range(2)]; o += 4 * KB
        ropeCb = [av(o + i * 2 * KB, 2 * KB, F32) for i in range(2)]; o += 4 * KB
        ropeSb = [av(o + i * 2 * KB, 2 * KB, F32) for i in range(2)]; o += 4 * KB
        rcb = [av(o + i * 2 * KB, 2 * KB, F32) for i in range(2)]; o += 4 * KB
        onesf = av(o, 256, F32); o += 256
        assert o <= ARENA_KB * KB, o

        wada_ring2 = [av(q_t1_off + i * 2 * KB, 2 * KB, BF16).rearrange("p (k c) -> p k c", k=8) for i in range(8)]
        P.op("pool", lambda e: e.memset(onesf, 1.0), [], ["onesf"])
        P.op("pool", lambda e: e.memset(jq2, 0.0), [], ["jq"])
        DMA("pool", rmb[0:2, :], rm_d[:, :], "rmb", [], ["rmb"])

        wr_ctr = [0]

        wr_pref = ["wr"]

        def load_w(src, kcn=8):
            i = wr_ctr[0] % len(wring)
            wr_ctr[0] += 1
            dst = wring[i] if kcn == 8 else wring[i][:, 0:kcn, :]
            k = f"{wr_pref[0]}{i}"
            DMA("pool", dst, src, k, [], [k])
            return wring[i], k

        def proj_fm(w, wkey, e0, n):
            bk, bkey = nextbank()
            for kc in range(8):
                MM(bk[:, :n], w[:, kc, :], hT[:, kc, e0:e0 + n], kc == 0, kc == 7, [wkey] + hkeys(e0, n), [bkey])
            return bk, bkey

        qk_ctr = [0]

        def qk_s0(it, st):
            i = qk_ctr[0] % 2
            st["i"] = i
            st["j"] = qk_ctr[0] % 3
            qk_ctr[0] += 1
            n = it["n"]
            st["bk"], st["bkey"] = proj_fm(it["w"], it["wkey"], it["e0"], n)
            ACTF(q_sq[i][:, :n], st["bk"][:, :n], AF.Square, [st["bkey"]], [f"qsq{i}"])

        def qk_s1(it, st):
            i, j, n, bk, bkey = st["i"], st["j"], it["n"], st["bk"], st["bkey"]
            b2, b2key = nextbank()
            MM(b2[:, :n], bones, q_sq[i][:, :n], True, True, [f"qsq{i}", "cst"], [b2key])
            ACTF(q_ln[i][:, :n], b2[:, :n], AF.Ln, [b2key], [f"qln{i}"], scale=1.0 / 64.0, bias=EPS)
            ACTF(q_rs[i][:, :n], q_ln[i][:, :n], AF.Exp, [f"qln{i}"], [f"qrs{i}"], scale=-0.5)
            gc = it["gcol"]
            if it["rot"]:
                STT(q_qn[j][:, :n], bk[:, :n], gq[:, gc:gc + 1], q_rs[i][:, :n], ALU.mult, ALU.mult,
                    [bkey, f"qrs{i}", "gq"], [f"qqn{j}"])
            else:
                for ps, oap in it["outs"]:
                    STT(oap, bk[ps, :n], gq[ps, gc:gc + 1], q_rs[i][ps, :n], ALU.mult, ALU.mult,
                        [bkey, f"qrs{i}", "gq"], it["okeys"])

        def qk_s2(it, st):
            if not it["rot"]:
                return
            i, n = st["i"], it["n"]
            DMA("sp", ropeCb[i][:, :n], ropeC_d[:, it["e0"]:it["e0"] + n], f"rc{i}", [], [f"rc{i}"])
            DMA("sp", ropeSb[i][:, :n], ropeS_d[:, it["e0"]:it["e0"] + n], f"rs{i}", [], [f"rs{i}"])

        def qk_s3(it, st):
            if not it["rot"]:
                return
            i, j, n, d = st["i"], st["j"], it["n"], it["d"]
            b3, b3key = nextbank()
            MM(b3[:, :n], rmat, q_qn[j][:, :n], True, True, [f"qqn{j}", "cst"], [b3key])
            TT("pool", q_t1[i][:, :n], q_qn[j][:, :n], ropeCb[i][:, :n], ALU.mult, [f"qqn{j}", f"rc{i}"], [f"qt1{i}"])
            TT("dve", q_t2[i][:, :n], b3[:, :n], ropeSb[i][:, :n], ALU.mult, [b3key, f"rs{i}"], [f"qt2{i}"])
            for hi, (ps, oap) in enumerate(it["outs"]):
                TT("dve", oap, q_t1[i][ps, :n].rearrange("p (a r) -> p a r", r=d),
                   q_t2[i][ps, :n].rearrange("p (a r) -> p a r", r=d), ALU.add, [f"qt1{i}", f"qt2{i}"], it["okeys"])

        def vtile(wv, wvkey, lhs_fn, hk, dst, vmcol):
            bk, bkey = nextbank()
            for kc in range(8):
                MM(bk[:, :128], lhs_fn(kc), wv[:, kc, :], kc == 0, kc == 7, [wvkey] + hk, [bkey])
            vm = vecs[:, V_VM + vmcol:V_VM + vmcol + 1]
            ACTF(dst.rearrange("p (b c) -> p b c", c=64)[:, 0:3:2, :], bk[:, :128].rearrange("p (b c) -> p b c", c=64),
                 AF.Copy, [bkey, "vecs"], ["jv"], scale=vm)

        def vones(vmbase, ntiles):
            src = vecs[:, V_VM + vmbase:V_VM + vmbase + ntiles].unsqueeze(2).to_broadcast([128, ntiles, 64])
            P.op("dve", lambda e: e.tensor_copy(out=jv[:, 0:ntiles, 64:128], in_=src), ["vecs"], ["jv"])

        at_ctr = [0]

        def at_s0(it, st):
            pi = at_ctr[0] % 2
            at_ctr[0] += 1
            st["pi"] = pi
            pt = ptb[pi]
            nq = it["nq"]
            per_bank = 512 // (2 * nq)
            banks = []
            for i, (kc0, vt) in enumerate(it["ktiles"]):
                if i % per_bank == 0:
                    banks.append(nextbank())
                bk, bkey = banks[-1]
                c0 = (i % per_bank) * 2 * nq
                dst = bk[:, c0:c0 + 2 * nq]
                sd = it.get("step", 1)
                q0 = it["qcols"]
                MM(dst.rearrange("p (h n) -> p h n", h=2), jk[:, kc0:kc0 + sd * 127 + 1:sd],
                   jq2[:, :, q0:q0 + sd * (nq - 1) + 1:sd], True, False, ["jk", "jq"], [bkey])
                it["extra"](i, dst, bkey)
            nt = len(it["ktiles"])
            for bi, (bk, bkey) in enumerate(banks):
                ntb = min(per_bank, nt - per_bank * bi)
                ACTF(pt[:, bi * 512:bi * 512 + ntb * 2 * nq], bk[:, :ntb * 2 * nq], AF.Exp, [bkey], [f"pt{pi}"])

        def at_s1(it, st):
            pi = st["pi"]
            pt = ptb[pi]
            nq = it["nq"]
            kts = it["ktiles"]
            nt = len(kts)
            bo, bokey = nextbank()
            for h in range(2):
                for i, (kc0, vt) in enumerate(kts):
                    lhs = jv[:, vt, 0:128] if h == 0 else jv[:, vt, 64:192]
                    c0 = i * 2 * nq + h * nq
                    MM(bo[:, h * nq:(h + 1) * nq], lhs, pt[:, c0:c0 + nq], i == 0, i == nt - 1, ["jv", f"pt{pi}"], [bokey])
            it["pvdst"](bo, bokey)

        nctr = [0]

        def normalize_to(bo_ap_fn, bokey, hl, n, dst, dkeys):
            i = nctr[0] % 2
            nctr[0] += 1
            rc = rcb[i]
            nu = slice(0, 64) if hl == 0 else slice(64, 128)
            de = slice(64, 128) if hl == 0 else slice(0, 64)
            P.op("dve", lambda e: e.reciprocal(out=rc[nu, 0:n], in_=bo_ap_fn(de)), [bokey], [f"rcb{i}"])
            TT("dve", dst, bo_ap_fn(nu), rc[nu, 0:n], ALU.mult, [bokey, f"rcb{i}"], dkeys)

        for hp in range(4):
            wq, wqk = load_w(win_d[hp])
            wk, wkk = load_w(win_d[4 + hp])
            wv, wvk = load_w(win_d[8 + hp])
            DMA("pool", stripk, strips_d[2 * hp:2 * hp + 2].rearrange("h t p c -> p h t c"), "stk", [], ["stk"])
            mod_dma(wada_ring2, 16 + 8 * hp, 24 + 8 * hp)
            items = []
            for c in range(4):
                sl = slice(c * 512, (c + 1) * 512)
                items.append(dict(w=wq, wkey=wqk, e0=1024 + 512 * c, n=512, gcol=0, rot=False, okeys=["jq"],
                                  outs=[(slice(0, 64), jq2[0:64, 0, sl]), (slice(64, 128), jq2[64:128, 1, sl])]))
            for (e0, n) in [(768, 256)] + [(1024 + 512 * c, 512) for c in range(4)] + [(3072, 256)]:
                items.append(dict(w=wk, wkey=wkk, e0=e0, n=n, gcol=1, rot=False, okeys=["jk"],
                                  outs=[(slice(0, 128), jk[:, e0 - 768:e0 - 768 + n])]))
            pipeline(items, [qk_s0, qk_s1])
            mb_, mbk_ = nextbank()
            mod_mm(wada_ring2, mb_, mbk_, 16 + 8 * hp, 24 + 8 * hp)
            if hp == 3:
                TS("dve", stmp2, modT[:, 32:40], 1.0, ALU.add, ["modT"], ["stmp2"])
                TT("dve", gsc2, stmp2, vecs[:, V_G2:V_G2 + 8], ALU.mult, ["stmp2", "vecs"], ["gsc"])
                if debug:
                    DMA("sp", dbg["mod"][:, :], modT[:, :], "dbg", ["modT"], [])
            vones(0, 20)
            for t in range(20):
                e0 = 768 + 128 * t
                vtile(wv, wvk, lambda kc, e0=e0: hT[:, kc, e0:e0 + 128], hkeys(e0, 128), jv[:, t, :], t)
            items = []
            for G in range(8):
                edge = G in (0, 7)

                def extra(i, dst, bkey, G=G, edge=edge):
                    i0 = 11 - 2 * i
                    ty = 1 if edge else 0
                    MM(dst.rearrange("p (h n) -> p h n", h=2), ident, stripk[:, :, ty, i0 * 64:i0 * 64 + 256],
                       False, not edge, ["stk", "cst"], [bkey])
                    if edge:
                        ix = (0 if G == 0 else 1) * 6 + i
                        for h in range(2):
                            MM(dst[:, h * 256:(h + 1) * 256], oh2, rmb[0:2, ix * 256:(ix + 1) * 256], False, h == 1,
                               ["rmb", "cst"], [bkey])

                def pvdst(bo, bokey, G=G, hp=hp):
                    for h in range(2):
                        normalize_to(lambda ps, h=h: bo[ps, h * 256:(h + 1) * 256], bokey, h, 256,
                                     oaT[64 * h:64 * h + 64, hp, G * 256:(G + 1) * 256], ["oaT"])

                items.append(dict(nq=256, qcols=G * 256, ktiles=[((2 * G + s_) * 128, 2 * G + s_) for s_ in range(6)],
                                  extra=extra, pvdst=pvdst))
            pipeline(items, [at_s0, at_s1])
        P.alias(["acc0", "acc1"], ["stk", "rmb"])
        P.alias(["qt10", "qt11", "qt20", "qt21", "rc0", "rc1", "rs0", "rs1"], [f"wa{i}" for i in range(8)])

        pending = []
        for hp in range(2):
            for gi, g in enumerate((0, 1, 2) if hp == 0 else (2, 0, 1)):
                d = DIL[g]
                L = 2048 // d
                Lx = L + 128
                ntl = Lx // 128
                ebase = 1024 - 64 * d
                vmbase = 20 + sum(DIL[gg] * ((2048 // DIL[gg] + 128) // 128) for gg in range(g))
                wq, wqk = load_w(win_d[12 + 2 * g + hp])
                wk, wkk = load_w(win_d[18 + 2 * g + hp])
                wv, wvk = load_w(win_d[24 + 2 * g + hp])
                items = []
                for c in range(4):
                    items.append(dict(w=wq, wkey=wqk, e0=1024 + 512 * c, n=512, gcol=2, rot=True, d=1, okeys=["jq"],
                                      outs=[(slice(64 * h, 64 * h + 64),
                                             jq2[64 * h:64 * h + 64, h, c * 512:(c + 1) * 512].unsqueeze(2))
                                            for h in range(2)]))
                tp = 0
                while tp < 2048 + 128 * d:
                    n = min(512, 2048 + 128 * d - tp)
                    items.append(dict(w=wk, wkey=wkk, e0=ebase + tp, n=n, gcol=3, rot=True, d=1, okeys=["jk"],
                                      outs=[(slice(0, 128), jk[:, tp:tp + n].unsqueeze(2))]))
                    tp += n
                pipeline(items, [qk_s0, qk_s1, qk_s2, qk_s3])
                vones(vmbase, d * ntl)
                for r in range(d):
                    for it_ in range(ntl):
                        es_ = ebase + r + 128 * d * it_
                        tix = r * ntl + it_
                        vtile(wv, wvk, lambda kc, es_=es_, d=d: hT[:, kc, es_:es_ + 127 * d + 1:d],
                              hkeys(es_, 127 * d + 1), jv[:, tix, :], vmbase + tix)
                        if pending and tix % 2 == 1:
                            pending.pop(0)()
                while pending:
                    pending.pop(0)()
                items = []
                for r in range(d):
                    for bb in range(L // 128):
                        def extra(i, dst, bkey):
                            MM(dst, ident, mk2[i], False, True, ["cst"], [bkey])

                        def pvdst(bo, bokey, r=r, bb=bb, d=d, gi=gi):
                            for h in range(2):
                                a3 = acc[h].rearrange("p (a r) -> p a r", r=d)[:, 128 * bb:128 * (bb + 1), r]
                                src = bo[:, h * 128:(h + 1) * 128]
                                if gi == 0:
                                    ACTF(a3, src, AF.Copy, [bokey], [f"acc{h}"])
                                else:
                                    TT("dve", a3, src, a3, ALU.add, [bokey, f"acc{h}"], [f"acc{h}"])

                        kts = [(r + d * 128 * (bb + kt), r * ntl + bb + kt) for kt in range(2)]
                        items.append(dict(nq=128, qcols=r + d * 128 * bb, ktiles=kts, extra=extra, pvdst=pvdst, step=d))
                pipeline(items, [at_s0, at_s1])
            def merge_piece(hl, c, hp=hp):
                i = nctr[0] % 2
                nctr[0] += 1
                rc = rcb[i]
                sl = slice(c * 512, (c + 1) * 512)
                nu = slice(0, 64) if hl == 0 else slice(64, 128)
                de = slice(64, 128) if hl == 0 else slice(0, 64)
                P.op("dve", lambda e: e.reciprocal(out=rc[nu, :], in_=acc[hl][de, sl]), [f"acc{hl}"], [f"rcb{i}"])
                TT("dve", obT[nu, hp, sl], acc[hl][nu, sl], rc[nu, :], ALU.mult, [f"acc{hl}", f"rcb{i}"], ["obT"])

            for c in range(4):
                for hl in range(2):
                    pending.append(lambda hl=hl, c=c, mp=merge_piece: mp(hl, c))
        if debug:
            while pending:
                pending.pop(0)()
            dh = av(S0, 8 * KB, F32)
            for k in range(4):
                P.op("dve", lambda e, k=k, dh=dh: e.tensor_copy(out=dh, in_=oaT[:, k, :]), ["oaT", "jq"], ["dh", "jq"])
                DMA("sp", dbg["oaT"][:, k, :], dh, "dbg", ["dh"], [])
            for k in range(2):
                P.op("dve", lambda e, k=k, dh=dh: e.tensor_copy(out=dh, in_=obT[:, k, :]), ["obT", "jq"], ["dh"])
                DMA("sp", dbg["obT"][:, k, :], dh, "dbg", ["dh"], [])
            P.barrier()

        mg = av(S0, 32 * KB, BF16).rearrange("p (k n) -> p k n", k=8)
        wring = [av(160 * KB + i * 2 * KB, 2 * KB, BF16).rearrange("p (k c) -> p k c", k=8) for i in range(8)]
        wr_pref[0] = "we"
        P.alias([f"mg{c}" for c in range(4)], ["jq", "jk", "jv", "acc0"])
        P.alias([f"we{i}" for i in range(8)],
                ["qln1", "qrs0", "qrs1", "qqn0", "qqn1", "qqn2", "qt10", "qt11", "qt20", "qt21"])
        P.alias(["gaf0", "gaf1", "gbf0", "gbf1"], ["wr2", "wr3", "wr4", "wr5"])
        P.alias(["t1f0", "t1f1", "t2f0", "t2f1"], ["pt0", "pt1"])
        P.alias([f"xr{i}" for i in range(4)], ["pt1", "qsq0", "qsq1", "qln0"])
        o = S0 + 48 * KB
        gaf = [av(o + i * 2 * KB, 2 * KB, F32) for i in range(2)]; o += 4 * KB
        gbf = [av(o + i * 2 * KB, 2 * KB, F32) for i in range(2)]; o += 4 * KB
        t1f = [av(o + i * 2 * KB, 2 * KB, F32) for i in range(2)]; o += 4 * KB
        t2f = [av(o + i * 2 * KB, 2 * KB, F32) for i in range(2)]; o += 4 * KB
        xres = [av(o + i * 2 * KB, 2 * KB, F32) for i in range(4)]; o += 8 * KB
        it = 0
        def ep1_loads(ot):
            return (load_w(win_d[30 + ot]), load_w(win_d[38 + ot]), load_w(wpa_d[ot], 4), load_w(wpb_d[ot], 2))

        nxt = ep1_loads(0)
        for ot in range(8):
            (wga, wgak), (wgb, wgbk), (wpa, wpak), (wpb, wpbk) = nxt
            if ot < 7:
                nxt = ep1_loads(ot + 1)
            for c in range(4):
                i = it % 2
                it += 1
                e0 = 1024 + 512 * c
                sl = slice(c * 512, (c + 1) * 512)
                if ot == 0:
                    for _ in range(2):
                        if pending:
                            pending.pop(0)()
                bk, bkey = proj_fm(wga, wgak, e0, 512)
                ACTF(gaf[i], bk[:, :], AF.Sigmoid, [bkey, "vecs"], [f"gaf{i}"], bias=vecs[:, V_BG + ot:V_BG + ot + 1])
                bk, bkey = proj_fm(wgb, wgbk, e0, 512)
                ACTF(gbf[i], bk[:, :], AF.Sigmoid, [bkey, "vecs"], [f"gbf{i}"], bias=vecs[:, V_BG + 8 + ot:V_BG + 9 + ot])
                bk, bkey = nextbank()
                for kc in range(4):
                    MM(bk[:, :], wpa[:, kc, :], oaT[:, kc, sl], kc == 0, kc == 3, [wpak, "oaT"], [bkey])
                TT("dve", t1f[i], bk[:, :], gaf[i], ALU.mult, [bkey, f"gaf{i}"], [f"t1f{i}"])
                bk, bkey = nextbank()
                for kc in range(2):
                    MM(bk[:, :], wpb[:, kc, :], obT[:, kc, sl], kc == 0, kc == 1, [wpbk, "obT"], [bkey])
                TT("dve", t2f[i], bk[:, :], gbf[i], ALU.mult, [bkey, f"gbf{i}"], [f"t2f{i}"])
                TT("pool", mg[:, ot, sl], t1f[i], t2f[i], ALU.add, [f"t1f{i}", f"t2f{i}"], [f"mg{c}"])
        P.barrier()

        x1 = av(0, 64 * KB, F32).rearrange("p (k n) -> p k n", k=8)
        h2 = av(160 * KB, 32 * KB, BF16).rearrange("p (k n) -> p k n", k=8)
        tb2 = av(64 * KB, 16 * KB, F32).rearrange("p (k n) -> p k n", k=8)
        nb2 = dict(sqb=[av(136 * KB + i * 8 * KB, 8 * KB, BF16).rearrange("p (k n) -> p k n", k=8) for i in range(2)],
                   tmpb=[tb2, tb2], tkeys=["tmpb0", "tmpb0"],
                   lnf=[av(80 * KB + i * 2 * KB, 2 * KB, F32) for i in range(2)])
        wring = [av(120 * KB + i * 2 * KB, 2 * KB, BF16).rearrange("p (k c) -> p k c", k=8) for i in range(8)]
        wr_pref[0] = "w2_"
        wos = [load_w(wo_d[ot]) for ot in range(8)]
        del bank_pool[:]
        bank_pool.extend([0, 1, 2, 3, 4])
        n2_bank_ctr = [0]

        def n2_bank():
            i = 5 + n2_bank_ctr[0] % 3
            n2_bank_ctr[0] += 1
            return pst[i], f"ps{i}"

        n2_items = []
        for c in range(4):
            n2_items.append(dict(xb=x1[:, :, c * 512:(c + 1) * 512], xkey=f"x1_{c}", gsc=gsc2, shcol=24, bufs=nb2,
                                 bankfn=n2_bank,
                                 dst_fn=lambda kc, c=c: h2[:, kc, c * 512:(c + 1) * 512], dkeys=[f"h2_{c}"]))
        n2_st = [dict() for _ in range(4)]
        n2_stages = [nrm_s0, nrm_s1, nrm_s2]
        it = 0
        for c in range(6):
            sl = slice(c * 512, (c + 1) * 512) if c < 4 else None
            for ot in range(8):
                if c < 4:
                    wo, wok = wos[ot]
                    i = it % 4
                    it += 1
                    DMA("sp", xres[i], xT[2 + c][:, ot, :], f"xr{i}", [], [f"xr{i}"])
                    bk, bkey = nextbank()
                    for kc in range(8):
                        MM(bk[:, :], wo[:, kc, :], mg[:, kc, sl], kc == 0, kc == 7, [wok, f"mg{c}"], [bkey])
                    STT(x1[:, ot, sl], bk[:, :], modT[:, 16 + ot:17 + ot], xres[i], ALU.mult, ALU.add,
                        [bkey, "modT", f"xr{i}"], [f"x1_{c}"])
                if ot == 1 and 0 <= c - 2 < 4:
                    nrm_s2(n2_items[c - 2], n2_st[c - 2])
                if ot == 5 and 0 <= c - 1 < 4:
                    nrm_s0(n2_items[c - 1], n2_st[c - 1])
                if ot == 7 and 0 <= c - 1 < 4:
                    nrm_s1(n2_items[c - 1], n2_st[c - 1])
        if debug:
            for c in range(4):
                DMA("sp", dbg["x1"][:, :, c * 512:(c + 1) * 512], x1[:, :, c * 512:(c + 1) * 512], "dbg", [f"x1_{c}"], [])
        del bank_pool[:]
        bank_pool.extend(range(8))
        ep_mg = [f"mg{c}" for c in range(4)]
        P.alias(["fq0a", "fq0u", "fq0o"] + [f"fq0a{j}" for j in range(6)] + [f"fq0u{j}" for j in range(6)],
                ep_mg + [f"w2_{i}" for i in range(8)])
        P.alias(["fq1a", "fq1u", "fq1o"] + [f"fq1a{j}" for j in range(6)] + [f"fq1u{j}" for j in range(6)],
                ["tmpb0", "lnf0", "lnf1"] + ep_mg)
        P.alias(["act0", "act1", "saf0", "saf1"], ["sqb0", "sqb1"])

        quarters = [list(range(0, 6)), list(range(6, 12)), list(range(12, 17)), list(range(17, 22))]
        wqb = []
        for qi in range(2):
            base = (100 if qi == 0 else 64) * KB
            wa_ = av(base, 12 * KB, BF16).rearrange("p (j k c) -> p j k c", j=6, k=8)
            wu_ = av(base + 12 * KB, 12 * KB, BF16).rearrange("p (j k c) -> p j k c", j=6, k=8)
            wo_ = av(base + 24 * KB, 12 * KB, BF16).rearrange("p (j c) -> p j c", j=6)
            wqb.append((wa_, wu_, wo_))
        actb = [av(136 * KB + i * 6 * KB, 6 * KB, BF16).rearrange("p (j n) -> p j n", j=6) for i in range(2)]
        saf = [av(148 * KB + i * 2 * KB, 2 * KB, F32) for i in range(2)]
        it = 0
        sit = 0
        for qi, hid in enumerate(quarters):
            wa_, wu_, wo_ = wqb[qi % 2]
            nh = len(hid)
            j0 = hid[0]
            kq = f"fq{qi % 2}"
            for jj in range(nh):
                DMA("pool", wa_[:, jj], wfi_d[j0 + jj], f"{kq}a{jj}", [], [f"{kq}a{jj}"])
                DMA("pool", wu_[:, jj], wfi_d[22 + j0 + jj], f"{kq}u{jj}", [], [f"{kq}u{jj}"])
            DMA("pool", wo_[:, 0:nh], wfo_d[j0:j0 + nh].rearrange("j p c -> p j c"), kq + "o", [], [kq + "o"])
            for c in range(4):
                sl = slice(c * 512, (c + 1) * 512)
                ab = actb[it % 2]
                abk = f"act{it % 2}"
                it += 1
                for jj in range(nh):
                    ba, bakey = nextbank()
                    for kc in range(8):
                        MM(ba[:, :], wa_[:, jj, kc, :], h2[:, kc, sl], kc == 0, kc == 7, [f"{kq}a{jj}", f"h2_{c}"], [bakey])
                    bu, bukey = nextbank()
                    for kc in range(8):
                        MM(bu[:, :], wu_[:, jj, kc, :], h2[:, kc, sl], kc == 0, kc == 7, [f"{kq}u{jj}", f"h2_{c}"], [bukey])
                    si = sit % 2
                    sit += 1
                    ACTF(saf[si], ba[:, :], AF.Silu, [bakey], [f"saf{si}"])
                    TT("dve", ab[:, jj, :], saf[si], bu[:, :], ALU.mult, [f"saf{si}", bukey], [abk])
                for ot in range(8):
                    by, bykey = nextbank()
                    for jj in range(nh):
                        MM(by[:, :], wo_[:, jj, ot * 128:(ot + 1) * 128], ab[:, jj, :], jj == 0, jj == nh - 1,
                           [kq + "o", abk], [bykey])
                    STT(x1[:, ot, sl], by[:, :], modT[:, 40 + ot:41 + ot], x1[:, ot, sl], ALU.mult, ALU.add,
                        [bykey, "modT", f"x1_{c}"], [f"x1_{c}"])
                if qi == 3:
                    DMA("sp", outT[:, :, sl], x1[:, :, sl], "out", [f"x1_{c}"], [])

        sems = {e: es.enter_context(nc.semaphore("s_" + e)) for e in Prog.ENGS}
        dsems = {k: es.enter_context(nc.semaphore("d_" + str(k))) for k in P.dma_keys}
        block = es.enter_context(nc.Block())
        P.emit(block, sems, dsems)
    return nc


_CACHE = {}


def _prep(inp):
    f = lambda a: np.asarray(a, dtype=np.float32)
    x = f(inp["x"])
    c = f(inp["c"])
    shared = {
        "cst": _constants(),
        "wada": _tile_w(f(inp["w_ada"])[0]),
        "win": _tile_w(f(inp["w_in"])[0]),
        "wpa": _tile_w(f(inp["w_proj_a"])[0]),
        "wpb": _tile_w(f(inp["w_proj_b"])[0]),
        "wo": _tile_w(f(inp["w_o"])[0]),
        "wfi": _tile_w(f(inp["w_ffn_in"])[0]),
        "wfo": np.ascontiguousarray(f(inp["w_ffn_out"])[0].reshape(22, 128, 1024)),
        "strips": _strips(f(inp["rpb"])[0]),
    }
    gq = np.stack([np.tile(f(inp[k])[0], 2) for k in ("g_qa", "g_ka", "g_qb", "g_kb")], axis=1)
    maps = []
    for core in range(8):
        b, q = core // 4, core % 4
        xe = np.zeros((4096, 1024), np.float32)
        lo = 2048 * q - 1024
        s0, s1 = max(lo, 0), min(lo + 4096, 8192)
        xe[s0 - lo:s1 - lo] = x[b, s0:s1]
        xTe = np.ascontiguousarray(xe.reshape(8, 512, 8, 128).transpose(0, 3, 2, 1))
        vec = np.zeros((128, NV), np.float32)
        vec[:, 0:8] = _col_vec(c[b])
        vec[:, 8:56] = _col_vec(f(inp["b_ada"])[0])
        vec[:, 56:64] = _col_vec(f(inp["g_norm1"])[0])
        vec[:, 64:72] = _col_vec(f(inp["g_norm2"])[0])
        vec[:, 72:88] = _col_vec(f(inp["b_gate"])[0])
        vec[:, 88:92] = gq
        vec[:, 92:181] = _vmask(q)
        C, S = _rope(q)
        m = dict(shared)
        m.update({"xT": xTe, "vecs": vec, "rm": _rowmask(q), "ropeC": C, "ropeS": S})
        maps.append(m)
    return maps


def kernel(**inputs):
    debug = bool(inputs.pop("_debug", False))
    key = ("nc", debug)
    if key not in _CACHE:
        _CACHE[key] = build_program(debug)
    nc = _CACHE[key]
    maps = _prep(inputs)
    res = run_bass_kernel_spmd(nc, maps, core_ids=list(range(8)))
    out = np.empty((2, 8192, 1024), np.float32)
    for core in range(8):
        b, q = core // 4, core % 4
        oT = res.results[core]["outT"]
        out[b, 2048 * q:2048 * (q + 1)] = oT.transpose(2, 1, 0).reshape(2048, 1024)
    if debug:
        kernel.last_results = res.results
    return out
```

```python
import contextlib
import numpy as np
import concourse.bass as bass
import concourse.mybir as mybir
from concourse.bass_utils import run_bass_kernel_spmd

F32 = mybir.dt.float32
BF16 = mybir.dt.bfloat16
AF = mybir.ActivationFunctionType
ALU = mybir.AluOpType
NEG = -30000.0
EPS = 1e-6
KB = 1024
DIL = (1, 4, 16)


class _Op:
    __slots__ = ("eng", "fn", "idx", "deps", "signal", "count", "dma_sem", "dma_val", "waits")

    def __init__(self, eng, fn, idx):
        self.eng = eng
        self.fn = fn
        self.idx = idx
        self.deps = set()
        self.signal = False
        self.count = 0
        self.dma_sem = None
        self.dma_val = 0
        self.waits = []


class Prog:
    ENGS = ("pe", "act", "dve", "pool", "sp")

    def __init__(self):
        self.ops = {e: [] for e in self.ENGS}
        self.writers = {}
        self.readers = {}
        self.dma_keys = {}
        self.last_dma = {}
        self.bar = set()

    def _add(self, eng, fn, reads, writes, semkey=None):
        o = _Op(eng, fn, len(self.ops[eng]))
        deps = set(self.bar)
        for r in reads:
            deps.update(self.writers.get(r, {}).values())
        for k in writes:
            deps.update(self.writers.get(k, {}).values())
            deps.update(self.readers.get(k, ()))
        deps.discard(o)
        o.deps = deps
        for r in reads:
            self.readers.setdefault(r, set()).add(o)
        wk = eng if semkey is None else ("dma", semkey)
        for k in writes:
            self.writers.setdefault(k, {})[wk] = o
            self.readers[k] = set()
        self.ops[eng].append(o)
        return o

    def op(self, eng, fn, reads=(), writes=()):
        return self._add(eng, fn, tuple(reads), tuple(writes))

    def dma(self, eng, fn, semkey, reads=(), writes=()):
        o = self._add(eng, fn, tuple(reads), tuple(writes), semkey)
        n = self.dma_keys.get(semkey, 0) + 1
        self.dma_keys[semkey] = n
        o.dma_sem = semkey
        o.dma_val = 16 * n
        self.last_dma[semkey] = o
        return o

    def alias(self, new_keys, old_keys):
        ws, rs = {}, set()
        for k in old_keys:
            for wk, o in self.writers.get(k, {}).items():
                cur = ws.get(wk)
                if cur is None or (o.dma_val if o.dma_sem is not None else o.idx) > \
                        (cur.dma_val if cur.dma_sem is not None else cur.idx):
                    ws[wk] = o
            rs |= self.readers.get(k, set())
        for nk in new_keys:
            d = self.writers.setdefault(nk, {})
            for wk, o in ws.items():
                cur = d.get(wk)
                if cur is None or (o.dma_val if o.dma_sem is not None else o.idx) > \
                        (cur.dma_val if cur.dma_sem is not None else cur.idx):
                    d[wk] = o
            self.readers.setdefault(nk, set()).update(rs)

    def barrier(self):
        b = set()
        for e in self.ENGS:
            for o in reversed(self.ops[e]):
                if o.dma_sem is None:
                    b.add(o)
                    break
        b.update(self.last_dma.values())
        self.bar = b
        self.writers = {}
        self.readers = {}

    def resolve(self):
        for X in self.ENGS:
            waited = {e: -1 for e in self.ENGS}
            waited_sem = {}
            for o in self.ops[X]:
                best = {}
                for d in o.deps:
                    if d.dma_sem is not None:
                        if waited_sem.get(d.dma_sem, 0) < d.dma_val:
                            k = ("s", d.dma_sem)
                            if k not in best or d.dma_val > best[k].dma_val:
                                best[k] = d
                    else:
                        if d.eng == X and X == "pe" and o.dma_sem is None:
                            continue
                        if d.idx > waited[d.eng]:
                            k = ("e", d.eng)
                            if k not in best or d.idx > best[k].idx:
                                best[k] = d
                for k, d in best.items():
                    if k[0] == "s":
                        waited_sem[d.dma_sem] = d.dma_val
                    else:
                        waited[d.eng] = d.idx
                        d.signal = True
                o.waits = list(best.values())
        for E in self.ENGS:
            c = 0
            for o in self.ops[E]:
                if o.dma_sem is None and o.signal:
                    c += 1
                    o.count = c

    def emit(self, block, sems, dsems):
        self.resolve()
        handles = {"pe": "tensor", "act": "scalar", "dve": "vector", "pool": "gpsimd", "sp": "sync"}
        prog = self

        def make(E):
            def body(eng):
                for o in prog.ops[E]:
                    for d in o.waits:
                        if d.dma_sem is not None:
                            eng.wait_ge(dsems[d.dma_sem], d.dma_val)
                        else:
                            eng.wait_ge(sems[d.eng], d.count)
                    ins = o.fn(eng)
                    if o.dma_sem is not None:
                        ins.then_inc(dsems[o.dma_sem], 16)
                    elif o.signal:
                        ins.then_inc(sems[E], 1)
                if E == "sp":
                    for k, n in prog.dma_keys.items():
                        eng.wait_ge(dsems[k], 16 * n)
            return body

        for E in self.ENGS:
            getattr(block, handles[E])(make(E))


def _tile_w(w):
    K, N = w.shape
    return np.ascontiguousarray(w.reshape(K // 128, 128, N // 128, 128).transpose(2, 1, 0, 3))


def _col_vec(v):
    return np.ascontiguousarray(v.reshape(-1, 128).T)


def _a_tiles(b):
    if b == 0:
        return list(range(0, 6))
    if b == 15:
        return list(range(14, 20))
    return list(range(b, b + 5))


EDGE_BLOCKS = (0, 1, 14, 15)
NV = 181


def _constants():
    c = np.zeros((128, 9 * 128), np.float32)
    idx = np.arange(128)
    c[:, 0:128] = np.eye(128, dtype=np.float32)
    c[:, 128:256] = 1.0
    c[:, 256:384] = (idx[:, None] // 64 == idx[None, :] // 64).astype(np.float32)
    r = np.zeros((128, 128), np.float32)
    for m in range(128):
        dd = m % 64
        if dd < 8:
            r[m + 8, m] = 1.0
        elif dd < 16:
            r[m - 8, m] = 1.0
    c[:, 384:512] = r
    k = idx[:, None]
    q = idx[None, :]
    mk0 = np.where(k >= q, 0.0, NEG)
    mk1 = np.where(k <= q, 0.0, NEG)
    c[:, 512:640] = mk0
    c[:, 640:768] = mk0
    c[:, 768:896] = mk1
    c[:, 896:1024] = mk1
    c[0, 1024:1088] = 1.0
    c[1, 1088:1152] = 1.0
    return c


def _strips(rpb):
    qc = np.arange(64)[None, :]
    kc = np.arange(64)[:, None]
    cs = np.clip(qc - 8, 0, 48)
    colmask = (kc >= cs) & (kc < cs + 16)
    coff = np.clip(kc - qc + 15, 0, 30)
    out = np.full((8, 2, 128, 16 * 64), NEG, np.float32)
    for ty in range(2):
        for half in range(2):
            for i in range(16):
                dr = 7 - i + half
                if abs(dr) > 7:
                    continue
                if ty == 0 and not (-4 <= dr <= 3):
                    continue
                g = rpb[:, dr + 7, :][:, coff]
                blk = np.where(colmask[None], g, np.float32(NEG))
                out[:, ty, half * 64:(half + 1) * 64, i * 64:(i + 1) * 64] = blk
    return out


def _rowmask(q):
    rm = np.full((2, 12 * 256), NEG, np.float32)
    for ge, G in enumerate((0, 7)):
        for s_ in range(6):
            t = 2 * G + s_
            ix = ge * 6 + s_
            for kl in range(2):
                kr = 32 * q + 2 * t - 4 + kl
                for j in range(4):
                    qr = 32 * q + 4 * G + j
                    rs = min(max(qr - 4, 0), 120)
                    if rs <= kr < rs + 8:
                        rm[kl, ix * 256 + j * 64: ix * 256 + (j + 1) * 64] = 0.0
    return rm


def _rope(q):
    pos = (2048 * q - 1024 + np.arange(4096)).astype(np.float32)
    inv = (np.float32(500000.0) ** (-(np.arange(8, dtype=np.float32) * np.float32(2.0)) / np.float32(16))).astype(np.float32)
    ang = (pos[:, None] * inv[None, :]).astype(np.float32).astype(np.float64)
    cs = np.cos(ang).astype(np.float32).T
    sn = np.sin(ang).astype(np.float32).T
    C = np.ones((128, 4096), np.float32)
    S = np.zeros((128, 4096), np.float32)
    for p in range(128):
        dd = p % 64
        if dd < 8:
            C[p] = cs[dd]
            S[p] = -sn[dd]
        elif dd < 16:
            C[p] = cs[dd - 8]
            S[p] = sn[dd - 8]
    return C, S


def _vmask(q):
    vm = np.ones((128, 89), np.float32)
    col = 20
    p = np.arange(128)
    for g, d in enumerate(DIL):
        lx = 2048 // d + 128
        nt = lx // 128
        for r in range(d):
            for i in range(nt):
                tok = 2048 * q - 64 * d + d * (128 * i + p) + r
                vm[:, col + r * nt + i] = ((tok >= 0) & (tok < 8192)).astype(np.float32)
        col += d * nt
    return vm


def build_program(debug=False):
    nc = bass.Bass("TRN2", target_bir_lowering=False)

    def din(name, shape):
        return nc.dram_tensor(name, shape, F32, kind="ExternalInput").ap()

    xT = din("xT", [8, 128, 8, 512])
    vecs_d = din("vecs", [128, NV])
    cst_d = din("cst", [128, 1152])
    wada_d = din("wada", [48, 128, 8, 128])
    win_d = din("win", [46, 128, 8, 128])
    wpa_d = din("wpa", [8, 128, 4, 128])
    wpb_d = din("wpb", [8, 128, 2, 128])
    wo_d = din("wo", [8, 128, 8, 128])
    wfi_d = din("wfi", [44, 128, 8, 128])
    wfo_d = din("wfo", [22, 128, 1024])
    strips_d = din("strips", [8, 2, 128, 1024])
    rm_d = din("rm", [2, 12 * 256])
    ropeC_d = din("ropeC", [128, 4096])
    ropeS_d = din("ropeS", [128, 4096])
    outT = nc.dram_tensor("outT", [128, 8, 2048], F32, kind="ExternalOutput").ap()
    dbg = {}
    if debug:
        dbg["hT"] = nc.dram_tensor("dbg_hT", [128, 8, 4096], F32, kind="ExternalOutput").ap()
        dbg["oaT"] = nc.dram_tensor("dbg_oaT", [128, 4, 2048], F32, kind="ExternalOutput").ap()
        dbg["obT"] = nc.dram_tensor("dbg_obT", [128, 2, 2048], F32, kind="ExternalOutput").ap()
        dbg["x1"] = nc.dram_tensor("dbg_x1", [128, 8, 2048], F32, kind="ExternalOutput").ap()
        dbg["mod"] = nc.dram_tensor("dbg_mod", [128, 48], F32, kind="ExternalOutput").ap()

    P = Prog()
    ARENA_KB = 192
    with contextlib.ExitStack() as es:
        arena = es.enter_context(nc.sbuf_tensor("arena", [128, ARENA_KB * KB // 2], BF16))
        cstb = es.enter_context(nc.sbuf_tensor("cstb", [128, 1152], BF16))
        vecs = es.enter_context(nc.sbuf_tensor("vecs_sb", [128, NV], F32))
        modT = es.enter_context(nc.sbuf_tensor("modT", [128, 48], F32))
        small = es.enter_context(nc.sbuf_tensor("small", [128, 64], F32))
        cactb_t = es.enter_context(nc.sbuf_tensor("cactb", [128, 16], BF16))
        pst = [es.enter_context(nc.psum_tensor(f"ps{i}", [128, 512], F32)) for i in range(8)]

        def av(off_b, nbytes, dt):
            a = arena[:, off_b // 2:(off_b + nbytes) // 2]
            return a if dt == BF16 else a.bitcast(F32)

        ident = cstb[:, 0:128]
        ones_b = cstb[:, 128:256]
        bones = cstb[:, 256:384]
        rmat = cstb[:, 384:512]
        mk2 = [cstb[:, 512:768], cstb[:, 768:1024]]
        oh2 = cstb[0:2, 1024:1152]

        V_C, V_BADA, V_G1, V_G2, V_BG, V_GQK, V_VM = 0, 8, 56, 64, 72, 88, 92
        gsc1 = small[:, 0:8]
        gsc2 = small[:, 8:16]
        gq = small[:, 16:20]
        cact2 = small[:, 24:40].rearrange("p (k n) -> p k n", n=2)
        stmp = small[:, 40:48]
        stmp2 = small[:, 48:56]

        bank_ctr = [0]
        bank_pool = [0, 1, 2, 3, 4, 5, 6]

        def nextbank():
            i = bank_pool[bank_ctr[0] % len(bank_pool)]
            bank_ctr[0] += 1
            return pst[i], f"ps{i}"

        def pipeline(items, stages):
            n, S = len(items), len(stages)
            st = [dict() for _ in items]
            for t in range(n + S - 1):
                for sidx in range(S):
                    i = t - sidx
                    if 0 <= i < n:
                        stages[sidx](items[i], st[i])

        def MM(out, lhsT, rhs, start, stop, r, w):
            P.op("pe", lambda e: e.matmul(out, lhsT=lhsT, rhs=rhs, start=start, stop=stop), r, w)

        def ACTF(out, in_, func, r, w, **kw):
            P.op("act", lambda e: e.activation(out=out, in_=in_, func=func, **kw), r, w)

        def TT(eng, out, in0, in1, op, r, w):
            P.op(eng, lambda e: e.tensor_tensor(out=out, in0=in0, in1=in1, op=op), r, w)

        def STT(out, in0, scalar, in1, op0, op1, r, w):
            P.op("dve", lambda e: e.scalar_tensor_tensor(out=out, in0=in0, scalar=scalar, in1=in1, op0=op0, op1=op1), r, w)

        def TS(eng, out, in0, s1, op0, r, w, s2=None, op1=None):
            if s2 is None:
                P.op(eng, lambda e: e.tensor_scalar(out=out, in0=in0, scalar1=s1, scalar2=None, op0=op0), r, w)
            else:
                P.op(eng, lambda e: e.tensor_scalar(out=out, in0=in0, scalar1=s1, scalar2=s2, op0=op0, op1=op1), r, w)

        def DMA(q, out, in_, key, r, w):
            P.dma(q, lambda e: e.dma_start(out=out, in_=in_), key, r, w)

        DMA("pool", cstb[:, :], cst_d[:, :], "cst", [], ["cst"])
        DMA("sp", vecs[:, :], vecs_d[:, :], "vecs", [], ["vecs"])
        cT = vecs[:, V_C:V_C + 8]
        ACTF(stmp, cT, AF.Exp, ["vecs"], ["stmp"], scale=-1.0)
        TS("dve", stmp, stmp, 1.0, ALU.add, ["stmp"], ["stmp"])
        P.op("dve", lambda e: e.reciprocal(out=stmp, in_=stmp), ["stmp"], ["stmp"])
        TT("dve", cact2[:, :, 0], cT, stmp, ALU.mult, ["vecs", "stmp"], ["cact"])
        TT("dve", cact2[:, :, 1], cT, stmp, ALU.mult, ["vecs", "stmp"], ["cact"])
        S0 = 88 * KB
        wada_ring = [av(64 * KB + i * 2 * KB, 2 * KB, BF16).rearrange("p (k c) -> p k c", k=8) for i in range(8)]
        cactb = cactb_t[:, :].rearrange("p (k n) -> p k n", n=2)
        P.op("dve", lambda e: e.tensor_copy(out=cactb, in_=cact2), ["cact"], ["cactb"])
        mbank, mkey = pst[7], "ps7"

        def mod_dma(ring, j0, j1):
            for j in range(j0, j1):
                DMA("pool", ring[j % 8], wada_d[j], f"wa{j % 8}", [], [f"wa{j % 8}"])

        def mod_mm(ring, bk, bkey, j0, j1):
            for j in range(j0, j1):
                for kc in range(8):
                    MM(bk[:, 2 * (j - j0):2 * (j - j0) + 2], ring[j % 8][:, kc, :], cactb[:, kc, :], kc == 0, kc == 7,
                       [f"wa{j % 8}", "cactb"], [bkey])
            mv = bk[:, 0:2 * (j1 - j0)].rearrange("p (j n) -> p j n", n=2)[:, :, 0]
            TT("dve", modT[:, j0:j1], mv, vecs[:, V_BADA + j0:V_BADA + j1], ALU.add, [bkey, "vecs"], ["modT"])

        mod_dma(wada_ring, 0, 8)
        mod_mm(wada_ring, mbank, mkey, 0, 8)
        mod_dma(wada_ring, 8, 16)
        mod_mm(wada_ring, mbank, mkey, 8, 16)
        TS("dve", stmp, modT[:, 8:16], 1.0, ALU.add, ["modT"], ["stmp"])
        TT("dve", gsc1, stmp, vecs[:, V_G1:V_G1 + 8], ALU.mult, ["stmp", "vecs"], ["gsc"])
        TS("dve", gq[:, :], vecs[:, V_GQK:V_GQK + 4], 1.0, ALU.mult, ["vecs"], ["gq"])
        TS("dve", gq[:, 0:1], vecs[:, V_GQK:V_GQK + 1], 0.125, ALU.mult, ["vecs", "gq"], ["gq"])
        TS("dve", gq[:, 2:3], vecs[:, V_GQK + 2:V_GQK + 3], 0.125, ALU.mult, ["vecs", "gq"], ["gq"])

        hT = av(0, 64 * KB, BF16).rearrange("p (k e) -> p k e", k=8)

        def hkeys(e0, n):
            return [f"hT{c}" for c in range(e0 // 512, (e0 + n - 1) // 512 + 1)]

        nrm_ctr = [0]

        def nrm_s0(it, st):
            i = nrm_ctr[0] % 2
            nrm_ctr[0] += 1
            st["i"] = i
            if it.get("pre") is not None:
                it["pre"]()
            sqb_ = it["bufs"]["sqb"][i]
            ACTF(sqb_, it["xb"], AF.Square, [it["xkey"]], [f"sqb{i}"])
            bk, bkey = it.get("bankfn", nextbank)()
            for kc in range(8):
                MM(bk[:, :], ones_b, sqb_[:, kc, :], kc == 0, kc == 7, [f"sqb{i}", "cst"], [bkey])
            st["bk"], st["bkey"] = bk, bkey

        def nrm_s1(it, st):
            i = st["i"]
            lnf_, tb = it["bufs"]["lnf"][i], it["bufs"]["tmpb"][i]
            tkey = it["bufs"].get("tkeys", ["tmpb0", "tmpb1"])[i]
            ACTF(lnf_, st["bk"][:, :], AF.Ln, [st["bkey"]], [f"lnf{i}"], scale=1.0 / 1024.0, bias=EPS)
            rsb, rskey = it.get("bankfn", nextbank)()
            ACTF(rsb[:, :], lnf_, AF.Exp, [f"lnf{i}"], [rskey], scale=-0.5)
            TT("dve", tb, it["xb"], rsb[:, :].unsqueeze(1).to_broadcast([128, 8, 512]), ALU.mult,
               [it["xkey"], rskey], [tkey])

        def nrm_s2(it, st):
            i = st["i"]
            tb = it["bufs"]["tmpb"][i]
            tkey = it["bufs"].get("tkeys", ["tmpb0", "tmpb1"])[i]
            gsc, shcol = it["gsc"], it["shcol"]
            for kc in range(8):
                sc_ap = gsc[:, kc:kc + 1]
                sh_ap = modT[:, shcol + kc:shcol + kc + 1]
                if kc % 2 == 0:
                    ACTF(it["dst_fn"](kc), tb[:, kc, :], AF.Identity, [tkey, "modT", "gsc"], it["dkeys"],
                         scale=sc_ap, bias=sh_ap)
                else:
                    TS("pool", it["dst_fn"](kc), tb[:, kc, :], sc_ap, ALU.mult, [tkey, "modT", "gsc"], it["dkeys"],
                       s2=sh_ap, op1=ALU.add)

        xbuf = [av(S0 + i * 16 * KB, 16 * KB, F32).rearrange("p (k n) -> p k n", k=8) for i in range(3)]
        nb1 = dict(sqb=[av(S0 + 48 * KB + i * 8 * KB, 8 * KB, BF16).rearrange("p (k n) -> p k n", k=8) for i in range(2)],
                   tmpb=[av(S0 + 64 * KB + i * 16 * KB, 16 * KB, F32).rearrange("p (k n) -> p k n", k=8) for i in range(2)],
                   lnf=[av(S0 + 96 * KB + i * 2 * KB, 2 * KB, F32) for i in range(2)])
        items = []
        for ec in range(8):
            def pre(ec=ec):
                DMA("sp", xbuf[ec % 3], xT[ec], f"xb{ec % 3}", ["modT"] if ec in (1, 2) else [], [f"xb{ec % 3}"])
            items.append(dict(xb=xbuf[ec % 3], xkey=f"xb{ec % 3}", gsc=gsc1, shcol=0, pre=pre, bufs=nb1,
                              dst_fn=lambda kc, ec=ec: hT[:, kc, ec * 512:(ec + 1) * 512], dkeys=[f"hT{ec}"]))
        pipeline(items, [nrm_s0, nrm_s1, nrm_s2])
        bank_pool.append(7)
        if debug:
            dh = av(S0 + 64 * KB, 16 * KB, F32).rearrange("p (k n) -> p k n", k=8)
            for ec in range(8):
                P.op("dve", lambda e, ec=ec, dh=dh: e.tensor_copy(out=dh, in_=hT[:, :, ec * 512:(ec + 1) * 512]), [f"hT{ec}"], ["dh", "tmpb0"])
                DMA("sp", dbg["hT"][:, :, ec * 512:(ec + 1) * 512], dh, "dbg", ["dh"], [])
            P.barrier()
        P.alias(["oaT", "obT"], [f"wa{i}" for i in range(8)])
        P.alias(["jq", "jk"], ["xb0"])
        P.alias(["jv"], ["xb1"])
        P.alias(["stk"], ["xb1", "xb2"])
        P.alias(["rmb", "wr0", "wr1"], ["xb2"])
        P.alias(["wr2", "wr3", "wr4", "wr5"], ["sqb0"])
        P.alias(["pt0", "pt1"], ["sqb1", "tmpb0"])
        P.alias(["qsq0", "qsq1", "qln0", "qln1", "qrs0", "qrs1", "qqn0", "qqn1", "qqn2"], ["tmpb0", "tmpb1"])
        P.alias([f"wa{i}" for i in range(8)], ["tmpb1", "lnf0"])
        P.alias(["rcb0", "rcb1", "onesf"], ["lnf0", "lnf1"])

        oaT = av(64 * KB, 16 * KB, BF16).rearrange("p (k n) -> p k n", k=4)
        obT = av(80 * KB, 8 * KB, BF16).rearrange("p (k n) -> p k n", k=2)
        o = S0
        jq2 = av(o, 8 * KB, BF16).rearrange("p (h n) -> p h n", h=2); o += 8 * KB
        jk = av(o, 8 * KB, BF16); o += 8 * KB
        jv = av(o, 12 * KB, BF16).rearrange("p (t c) -> p t c", c=192); o += 12 * KB
        acc = [av(o + i * 8 * KB, 8 * KB, F32) for i in range(2)]
        stripk = av(o, 8 * KB, BF16).rearrange("p (h t c) -> p h t c", h=2, t=2)
        rmb = av(o + 8 * KB, 6 * KB, BF16)
        o += 16 * KB
        wring = [av(o + i * 2 * KB, 2 * KB, BF16).rearrange("p (k c) -> p k c", k=8) for i in range(6)]; o += 12 * KB
        ptb = [av(o + i * 6 * KB, 6 * KB, BF16) for i in range(2)]; o += 12 * KB
        q_sq = [av(o + i * KB, KB, BF16) for i in range(2)]; o += 2 * KB
        q_ln = [av(o + i * 2 * KB, 2 * KB, F32) for i in range(2)]; o += 4 * KB
        q_rs = [av(o + i * 2 * KB, 2 * KB, F32) for i in range(2)]; o += 4 * KB
        q_qn = [av(o + i * KB, KB, BF16) for i in range(3)]; o += 3 * KB
        q_t1_off = o
        q_t1 = [av(o + i * 2 * KB, 2 * KB, F32) for i in range(2)]; o += 4 * KB
        q_t2 = [av(o + i * 2 * KB, 2 * KB, F32) for i in range(2)]; o += 4 * KB
        ropeCb = [av(o + i * 2 * KB, 2 * KB, F32) for i in range(2)]; o += 4 * KB
        ropeSb = [av(o + i * 2 * KB, 2 * KB, F32) for i in range(2)]; o += 4 * KB
        rcb = [av(o + i * 2 * KB, 2 * KB, F32) for i in range(2)]; o += 4 * KB
        onesf = av(o, 256, F32); o += 256
        assert o <= ARENA_KB * KB, o

        wada_ring2 = [av(q_t1_off + i * 2 * KB, 2 * KB, BF16).rearrange("p (k c) -> p k c", k=8) for i in range(8)]
        P.op("pool", lambda e: e.memset(onesf, 1.0), [], ["onesf"])
        P.op("pool", lambda e: e.memset(jq2, 0.0), [], ["jq"])
        DMA("pool", rmb[0:2, :], rm_d[:, :], "rmb", [], ["rmb"])

        wr_ctr = [0]

        wr_pref = ["wr"]

        def load_w(src, kcn=8):
            i = wr_ctr[0] % len(wring)
            wr_ctr[0] += 1
            dst = wring[i] if kcn == 8 else wring[i][:, 0:kcn, :]
            k = f"{wr_pref[0]}{i}"
            DMA("pool", dst, src, k, [], [k])
            return wring[i], k

        def proj_fm(w, wkey, e0, n):
            bk, bkey = nextbank()
            for kc in range(8):
                MM(bk[:, :n], w[:, kc, :], hT[:, kc, e0:e0 + n], kc == 0, kc == 7, [wkey] + hkeys(e0, n), [bkey])
            return bk, bkey

        qk_ctr = [0]

        def qk_s0(it, st):
            i = qk_ctr[0] % 2
            st["i"] = i
            st["j"] = qk_ctr[0] % 3
            qk_ctr[0] += 1
            n = it["n"]
            st["bk"], st["bkey"] = proj_fm(it["w"], it["wkey"], it["e0"], n)
            ACTF(q_sq[i][:, :n], st["bk"][:, :n], AF.Square, [st["bkey"]], [f"qsq{i}"])

        def qk_s1(it, st):
            i, j, n, bk, bkey = st["i"], st["j"], it["n"], st["bk"], st["bkey"]
            b2, b2key = nextbank()
            MM(b2[:, :n], bones, q_sq[i][:, :n], True, True, [f"qsq{i}", "cst"], [b2key])
            ACTF(q_ln[i][:, :n], b2[:, :n], AF.Ln, [b2key], [f"qln{i}"], scale=1.0 / 64.0, bias=EPS)
            ACTF(q_rs[i][:, :n], q_ln[i][:, :n], AF.Exp, [f"qln{i}"], [f"qrs{i}"], scale=-0.5)
            gc = it["gcol"]
            if it["rot"]:
                STT(q_qn[j][:, :n], bk[:, :n], gq[:, gc:gc + 1], q_rs[i][:, :n], ALU.mult, ALU.mult,
                    [bkey, f"qrs{i}", "gq"], [f"qqn{j}"])
            else:
                for ps, oap in it["outs"]:
                    STT(oap, bk[ps, :n], gq[ps, gc:gc + 1], q_rs[i][ps, :n], ALU.mult, ALU.mult,
                        [bkey, f"qrs{i}", "gq"], it["okeys"])

        def qk_s2(it, st):
            if not it["rot"]:
                return
            i, n = st["i"], it["n"]
            DMA("sp", ropeCb[i][:, :n], ropeC_d[:, it["e0"]:it["e0"] + n], f"rc{i}", [], [f"rc{i}"])
            DMA("sp", ropeSb[i][:, :n], ropeS_d[:, it["e0"]:it["e0"] + n], f"rs{i}", [], [f"rs{i}"])

        def qk_s3(it, st):
            if not it["rot"]:
                return
            i, j, n, d = st["i"], st["j"], it["n"], it["d"]
            b3, b3key = nextbank()
            MM(b3[:, :n], rmat, q_qn[j][:, :n], True, True, [f"qqn{j}", "cst"], [b3key])
            TT("pool", q_t1[i][:, :n], q_qn[j][:, :n], ropeCb[i][:, :n], ALU.mult, [f"qqn{j}", f"rc{i}"], [f"qt1{i}"])
            TT("dve", q_t2[i][:, :n], b3[:, :n], ropeSb[i][:, :n], ALU.mult, [b3key, f"rs{i}"], [f"qt2{i}"])
            for hi, (ps, oap) in enumerate(it["outs"]):
                TT("dve", oap, q_t1[i][ps, :n].rearrange("p (a r) -> p a r", r=d),
                   q_t2[i][ps, :n].rearrange("p (a r) -> p a r", r=d), ALU.add, [f"qt1{i}", f"qt2{i}"], it["okeys"])

        def vtile(wv, wvkey, lhs_fn, hk, dst, vmcol):
            bk, bkey = nextbank()
            for kc in range(8):
                MM(bk[:, :128], lhs_fn(kc), wv[:, kc, :], kc == 0, kc == 7, [wvkey] + hk, [bkey])
            vm = vecs[:, V_VM + vmcol:V_VM + vmcol + 1]
            ACTF(dst.rearrange("p (b c) -> p b c", c=64)[:, 0:3:2, :], bk[:, :128].rearrange("p (b c) -> p b c", c=64),
                 AF.Copy, [bkey, "vecs"], ["jv"], scale=vm)

        def vones(vmbase, ntiles):
            src = vecs[:, V_VM + vmbase:V_VM + vmbase + ntiles].unsqueeze(2).to_broadcast([128, ntiles, 64])
            P.op("dve", lambda e: e.tensor_copy(out=jv[:, 0:ntiles, 64:128], in_=src), ["vecs"], ["jv"])

        at_ctr = [0]

        def at_s0(it, st):
            pi = at_ctr[0] % 2
            at_ctr[0] += 1
            st["pi"] = pi
            pt = ptb[pi]
            nq = it["nq"]
            per_bank = 512 // (2 * nq)
            banks = []
            for i, (kc0, vt) in enumerate(it["ktiles"]):
                if i % per_bank == 0:
                    banks.append(nextbank())
                bk, bkey = banks[-1]
                c0 = (i % per_bank) * 2 * nq
                dst = bk[:, c0:c0 + 2 * nq]
                sd = it.get("step", 1)
                q0 = it["qcols"]
                MM(dst.rearrange("p (h n) -> p h n", h=2), jk[:, kc0:kc0 + sd * 127 + 1:sd],
                   jq2[:, :, q0:q0 + sd * (nq - 1) + 1:sd], True, False, ["jk", "jq"], [bkey])
                it["extra"](i, dst, bkey)
            nt = len(it["ktiles"])
            for bi, (bk, bkey) in enumerate(banks):
                ntb = min(per_bank, nt - per_bank * bi)
                ACTF(pt[:, bi * 512:bi * 512 + ntb * 2 * nq], bk[:, :ntb * 2 * nq], AF.Exp, [bkey], [f"pt{pi}"])

        def at_s1(it, st):
            pi = st["pi"]
            pt = ptb[pi]
            nq = it["nq"]
            kts = it["ktiles"]
            nt = len(kts)
            bo, bokey = nextbank()
            for h in range(2):
                for i, (kc0, vt) in enumerate(kts):
                    lhs = jv[:, vt, 0:128] if h == 0 else jv[:, vt, 64:192]
                    c0 = i * 2 * nq + h * nq
                    MM(bo[:, h * nq:(h + 1) * nq], lhs, pt[:, c0:c0 + nq], i == 0, i == nt - 1, ["jv", f"pt{pi}"], [bokey])
            it["pvdst"](bo, bokey)

        nctr = [0]

        def normalize_to(bo_ap_fn, bokey, hl, n, dst, dkeys):
            i = nctr[0] % 2
            nctr[0] += 1
            rc = rcb[i]
            nu = slice(0, 64) if hl == 0 else slice(64, 128)
            de = slice(64, 128) if hl == 0 else slice(0, 64)
            P.op("dve", lambda e: e.reciprocal(out=rc[nu, 0:n], in_=bo_ap_fn(de)), [bokey], [f"rcb{i}"])
            TT("dve", dst, bo_ap_fn(nu), rc[nu, 0:n], ALU.mult, [bokey, f"rcb{i}"], dkeys)

        for hp in range(4):
            wq, wqk = load_w(win_d[hp])
            wk, wkk = load_w(win_d[4 + hp])
            wv, wvk = load_w(win_d[8 + hp])
            DMA("pool", stripk, strips_d[2 * hp:2 * hp + 2].rearrange("h t p c -> p h t c"), "stk", [], ["stk"])
            mod_dma(wada_ring2, 16 + 8 * hp, 24 + 8 * hp)
            items = []
            for c in range(4):
                sl = slice(c * 512, (c + 1) * 512)
                items.append(dict(w=wq, wkey=wqk, e0=1024 + 512 * c, n=512, gcol=0, rot=False, okeys=["jq"],
                                  outs=[(slice(0, 64), jq2[0:64, 0, sl]), (slice(64, 128), jq2[64:128, 1, sl])]))
            for (e0, n) in [(768, 256)] + [(1024 + 512 * c, 512) for c in range(4)] + [(3072, 256)]:
                items.append(dict(w=wk, wkey=wkk, e0=e0, n=n, gcol=1, rot=False, okeys=["jk"],
                                  outs=[(slice(0, 128), jk[:, e0 - 768:e0 - 768 + n])]))
            pipeline(items, [qk_s0, qk_s1])
            mb_, mbk_ = nextbank()
            mod_mm(wada_ring2, mb_, mbk_, 16 + 8 * hp, 24 + 8 * hp)
            if hp == 3:
                TS("dve", stmp2, modT[:, 32:40], 1.0, ALU.add, ["modT"], ["stmp2"])
                TT("dve", gsc2, stmp2, vecs[:, V_G2:V_G2 + 8], ALU.mult, ["stmp2", "vecs"], ["gsc"])
                if debug:
                    DMA("sp", dbg["mod"][:, :], modT[:, :], "dbg", ["modT"], [])
            vones(0, 20)
            for t in range(20):
                e0 = 768 + 128 * t
                vtile(wv, wvk, lambda kc, e0=e0: hT[:, kc, e0:e0 + 128], hkeys(e0, 128), jv[:, t, :], t)
            items = []
            for G in range(8):
                edge = G in (0, 7)

                def extra(i, dst, bkey, G=G, edge=edge):
                    i0 = 11 - 2 * i
                    ty = 1 if edge else 0
                    MM(dst.rearrange("p (h n) -> p h n", h=2), ident, stripk[:, :, ty, i0 * 64:i0 * 64 + 256],
                       False, not edge, ["stk", "cst"], [bkey])
                    if edge:
                        ix = (0 if G == 0 else 1) * 6 + i
                        for h in range(2):
                            MM(dst[:, h * 256:(h + 1) * 256], oh2, rmb[0:2, ix * 256:(ix + 1) * 256], False, h == 1,
                               ["rmb", "cst"], [bkey])

                def pvdst(bo, bokey, G=G, hp=hp):
                    for h in range(2):
                        normalize_to(lambda ps, h=h: bo[ps, h * 256:(h + 1) * 256], bokey, h, 256,
                                     oaT[64 * h:64 * h + 64, hp, G * 256:(G + 1) * 256], ["oaT"])

                items.append(dict(nq=256, qcols=G * 256, ktiles=[((2 * G + s_) * 128, 2 * G + s_) for s_ in range(6)],
                                  extra=extra, pvdst=pvdst))
            pipeline(items, [at_s0, at_s1])
        P.alias(["acc0", "acc1"], ["stk", "rmb"])
        P.alias(["qt10", "qt11", "qt20", "qt21", "rc0", "rc1", "rs0", "rs1"], [f"wa{i}" for i in range(8)])

        pending = []
        for hp in range(2):
            for gi, g in enumerate((0, 1, 2) if hp == 0 else (2, 0, 1)):
                d = DIL[g]
                L = 2048 // d
                Lx = L + 128
                ntl = Lx // 128
                ebase = 1024 - 64 * d
                vmbase = 20 + sum(DIL[gg] * ((2048 // DIL[gg] + 128) // 128) for gg in range(g))
                wq, wqk = load_w(win_d[12 + 2 * g + hp])
                wk, wkk = load_w(win_d[18 + 2 * g + hp])
                wv, wvk = load_w(win_d[24 + 2 * g + hp])
                items = []
                for c in range(4):
                    items.append(dict(w=wq, wkey=wqk, e0=1024 + 512 * c, n=512, gcol=2, rot=True, d=1, okeys=["jq"],
                                      outs=[(slice(64 * h, 64 * h + 64),
                                             jq2[64 * h:64 * h + 64, h, c * 512:(c + 1) * 512].unsqueeze(2))
                                            for h in range(2)]))
                tp = 0
                while tp < 2048 + 128 * d:
                    n = min(512, 2048 + 128 * d - tp)
                    items.append(dict(w=wk, wkey=wkk, e0=ebase + tp, n=n, gcol=3, rot=True, d=1, okeys=["jk"],
                                      outs=[(slice(0, 128), jk[:, tp:tp + n].unsqueeze(2))]))
                    tp += n
                pipeline(items, [qk_s0, qk_s1, qk_s2, qk_s3])
                vones(vmbase, d * ntl)
                for r in range(d):
                    for it_ in range(ntl):
                        es_ = ebase + r + 128 * d * it_
                        tix = r * ntl + it_
                        vtile(wv, wvk, lambda kc, es_=es_, d=d: hT[:, kc, es_:es_ + 127 * d + 1:d],
                              hkeys(es_, 127 * d + 1), jv[:, tix, :], vmbase + tix)
                        if pending and tix % 2 == 1:
                            pending.pop(0)()
                while pending:
                    pending.pop(0)()
                items = []
                for r in range(d):
                    for bb in range(L // 128):
                        def extra(i, dst, bkey):
                            MM(dst, ident, mk2[i], False, True, ["cst"], [bkey])

                        def pvdst(bo, bokey, r=r, bb=bb, d=d, gi=gi):
                            for h in range(2):
                                a3 = acc[h].rearrange("p (a r) -> p a r", r=d)[:, 128 * bb:128 * (bb + 1), r]
                                src = bo[:, h * 128:(h + 1) * 128]
                                if gi == 0:
                                    ACTF(a3, src, AF.Copy, [bokey], [f"acc{h}"])
                                else:
                                    TT("dve", a3, src, a3, ALU.add, [bokey, f"acc{h}"], [f"acc{h}"])

                        kts = [(r + d * 128 * (bb + kt), r * ntl + bb + kt) for kt in range(2)]
                        items.append(dict(nq=128, qcols=r + d * 128 * bb, ktiles=kts, extra=extra, pvdst=pvdst, step=d))
                pipeline(items, [at_s0, at_s1])
            def merge_piece(hl, c, hp=hp):
                i = nctr[0] % 2
                nctr[0] += 1
                rc = rcb[i]
                sl = slice(c * 512, (c + 1) * 512)
                nu = slice(0, 64) if hl == 0 else slice(64, 128)
                de = slice(64, 128) if hl == 0 else slice(0, 64)
                P.op("dve", lambda e: e.reciprocal(out=rc[nu, :], in_=acc[hl][de, sl]), [f"acc{hl}"], [f"rcb{i}"])
                TT("dve", obT[nu, hp, sl], acc[hl][nu, sl], rc[nu, :], ALU.mult, [f"acc{hl}", f"rcb{i}"], ["obT"])

            for c in range(4):
                for hl in range(2):
                    pending.append(lambda hl=hl, c=c, mp=merge_piece: mp(hl, c))
        if debug:
            while pending:
                pending.pop(0)()
            dh = av(S0, 8 * KB, F32)
            for k in range(4):
                P.op("dve", lambda e, k=k, dh=dh: e.tensor_copy(out=dh, in_=oaT[:, k, :]), ["oaT", "jq"], ["dh", "jq"])
                DMA("sp", dbg["oaT"][:, k, :], dh, "dbg", ["dh"], [])
            for k in range(2):
                P.op("dve", lambda e, k=k, dh=dh: e.tensor_copy(out=dh, in_=obT[:, k, :]), ["obT", "jq"], ["dh"])
                DMA("sp", dbg["obT"][:, k, :], dh, "dbg", ["dh"], [])
            P.barrier()

        mg = av(S0, 32 * KB, BF16).rearrange("p (k n) -> p k n", k=8)
        wring = [av(160 * KB + i * 2 * KB, 2 * KB, BF16).rearrange("p (k c) -> p k c", k=8) for i in range(8)]
        wr_pref[0] = "we"
        P.alias([f"mg{c}" for c in range(4)], ["jq", "jk", "jv", "acc0"])
        P.alias([f"we{i}" for i in range(8)],
                ["qln1", "qrs0", "qrs1", "qqn0", "qqn1", "qqn2", "qt10", "qt11", "qt20", "qt21"])
        P.alias(["gaf0", "gaf1", "gbf0", "gbf1"], ["wr2", "wr3", "wr4", "wr5"])
        P.alias(["t1f0", "t1f1", "t2f0", "t2f1"], ["pt0", "pt1"])
        P.alias([f"xr{i}" for i in range(4)], ["pt1", "qsq0", "qsq1", "qln0"])
        o = S0 + 48 * KB
        gaf = [av(o + i * 2 * KB, 2 * KB, F32) for i in range(2)]; o += 4 * KB
        gbf = [av(o + i * 2 * KB, 2 * KB, F32) for i in range(2)]; o += 4 * KB
        t1f = [av(o + i * 2 * KB, 2 * KB, F32) for i in range(2)]; o += 4 * KB
        t2f = [av(o + i * 2 * KB, 2 * KB, F32) for i in range(2)]; o += 4 * KB
        xres = [av(o + i * 2 * KB, 2 * KB, F32) for i in range(4)]; o += 8 * KB
        it = 0
        def ep1_loads(ot):
            return (load_w(win_d[30 + ot]), load_w(win_d[38 + ot]), load_w(wpa_d[ot], 4), load_w(wpb_d[ot], 2))

        nxt = ep1_loads(0)
        for ot in range(8):
            (wga, wgak), (wgb, wgbk), (wpa, wpak), (wpb, wpbk) = nxt
            if ot < 7:
                nxt = ep1_loads(ot + 1)
            for c in range(4):
                i = it % 2
                it += 1
                e0 = 1024 + 512 * c
                sl = slice(c * 512, (c + 1) * 512)
                if ot == 0:
                    for _ in range(2):
                        if pending:
                            pending.pop(0)()
                bk, bkey = proj_fm(wga, wgak, e0, 512)
                ACTF(gaf[i], bk[:, :], AF.Sigmoid, [bkey, "vecs"], [f"gaf{i}"], bias=vecs[:, V_BG + ot:V_BG + ot + 1])
                bk, bkey = proj_fm(wgb, wgbk, e0, 512)
                ACTF(gbf[i], bk[:, :], AF.Sigmoid, [bkey, "vecs"], [f"gbf{i}"], bias=vecs[:, V_BG + 8 + ot:V_BG + 9 + ot])
                bk, bkey = nextbank()
                for kc in range(4):
                    MM(bk[:, :], wpa[:, kc, :], oaT[:, kc, sl], kc == 0, kc == 3, [wpak, "oaT"], [bkey])
                TT("dve", t1f[i], bk[:, :], gaf[i], ALU.mult, [bkey, f"gaf{i}"], [f"t1f{i}"])
                bk, bkey = nextbank()
                for kc in range(2):
                    MM(bk[:, :], wpb[:, kc, :], obT[:, kc, sl], kc == 0, kc == 1, [wpbk, "obT"], [bkey])
                TT("dve", t2f[i], bk[:, :], gbf[i], ALU.mult, [bkey, f"gbf{i}"], [f"t2f{i}"])
                TT("pool", mg[:, ot, sl], t1f[i], t2f[i], ALU.add, [f"t1f{i}", f"t2f{i}"], [f"mg{c}"])
        P.barrier()

        x1 = av(0, 64 * KB, F32).rearrange("p (k n) -> p k n", k=8)
        h2 = av(160 * KB, 32 * KB, BF16).rearrange("p (k n) -> p k n", k=8)
        tb2 = av(64 * KB, 16 * KB, F32).rearrange("p (k n) -> p k n", k=8)
        nb2 = dict(sqb=[av(136 * KB + i * 8 * KB, 8 * KB, BF16).rearrange("p (k n) -> p k n", k=8) for i in range(2)],
                   tmpb=[tb2, tb2], tkeys=["tmpb0", "tmpb0"],
                   lnf=[av(80 * KB + i * 2 * KB, 2 * KB, F32) for i in range(2)])
        wring = [av(120 * KB + i * 2 * KB, 2 * KB, BF16).rearrange("p (k c) -> p k c", k=8) for i in range(8)]
        wr_pref[0] = "w2_"
        wos = [load_w(wo_d[ot]) for ot in range(8)]
        del bank_pool[:]
        bank_pool.extend([0, 1, 2, 3, 4])
        n2_bank_ctr = [0]

        def n2_bank():
            i = 5 + n2_bank_ctr[0] % 3
            n2_bank_ctr[0] += 1
            return pst[i], f"ps{i}"

        n2_items = []
        for c in range(4):
            n2_items.append(dict(xb=x1[:, :, c * 512:(c + 1) * 512], xkey=f"x1_{c}", gsc=gsc2, shcol=24, bufs=nb2,
                                 bankfn=n2_bank,
                                 dst_fn=lambda kc, c=c: h2[:, kc, c * 512:(c + 1) * 512], dkeys=[f"h2_{c}"]))
        n2_st = [dict() for _ in range(4)]
        n2_stages = [nrm_s0, nrm_s1, nrm_s2]
        it = 0
        for c in range(6):
            sl = slice(c * 512, (c + 1) * 512) if c < 4 else None
            for ot in range(8):
                if c < 4:
                    wo, wok = wos[ot]
                    i = it % 4
                    it += 1
                    DMA("sp", xres[i], xT[2 + c][:, ot, :], f"xr{i}", [], [f"xr{i}"])
                    bk, bkey = nextbank()
                    for kc in range(8):
                        MM(bk[:, :], wo[:, kc, :], mg[:, kc, sl], kc == 0, kc == 7, [wok, f"mg{c}"], [bkey])
                    STT(x1[:, ot, sl], bk[:, :], modT[:, 16 + ot:17 + ot], xres[i], ALU.mult, ALU.add,
                        [bkey, "modT", f"xr{i}"], [f"x1_{c}"])
                if ot == 1 and 0 <= c - 2 < 4:
                    nrm_s2(n2_items[c - 2], n2_st[c - 2])
                if ot == 5 and 0 <= c - 1 < 4:
                    nrm_s0(n2_items[c - 1], n2_st[c - 1])
                if ot == 7 and 0 <= c - 1 < 4:
                    nrm_s1(n2_items[c - 1], n2_st[c - 1])
        if debug:
            for c in range(4):
                DMA("sp", dbg["x1"][:, :, c * 512:(c + 1) * 512], x1[:, :, c * 512:(c + 1) * 512], "dbg", [f"x1_{c}"], [])
        del bank_pool[:]
        bank_pool.extend(range(8))
        ep_mg = [f"mg{c}" for c in range(4)]
        P.alias(["fq0a", "fq0u", "fq0o"] + [f"fq0a{j}" for j in range(6)] + [f"fq0u{j}" for j in range(6)],
                ep_mg + [f"w2_{i}" for i in range(8)])
        P.alias(["fq1a", "fq1u", "fq1o"] + [f"fq1a{j}" for j in range(6)] + [f"fq1u{j}" for j in range(6)],
                ["tmpb0", "lnf0", "lnf1"] + ep_mg)
        P.alias(["act0", "act1", "saf0", "saf1"], ["sqb0", "sqb1"])

        quarters = [list(range(0, 6)), list(range(6, 12)), list(range(12, 17)), list(range(17, 22))]
        wqb = []
        for qi in range(2):
            base = (100 if qi == 0 else 64) * KB
            wa_ = av(base, 12 * KB, BF16).rearrange("p (j k c) -> p j k c", j=6, k=8)
            wu_ = av(base + 12 * KB, 12 * KB, BF16).rearrange("p (j k c) -> p j k c", j=6, k=8)
            wo_ = av(base + 24 * KB, 12 * KB, BF16).rearrange("p (j c) -> p j c", j=6)
            wqb.append((wa_, wu_, wo_))
        actb = [av(136 * KB + i * 6 * KB, 6 * KB, BF16).rearrange("p (j n) -> p j n", j=6) for i in range(2)]
        saf = [av(148 * KB + i * 2 * KB, 2 * KB, F32) for i in range(2)]
        it = 0
        sit = 0
        for qi, hid in enumerate(quarters):
            wa_, wu_, wo_ = wqb[qi % 2]
            nh = len(hid)
            j0 = hid[0]
            kq = f"fq{qi % 2}"
            for jj in range(nh):
                DMA("pool", wa_[:, jj], wfi_d[j0 + jj], f"{kq}a{jj}", [], [f"{kq}a{jj}"])
                DMA("pool", wu_[:, jj], wfi_d[22 + j0 + jj], f"{kq}u{jj}", [], [f"{kq}u{jj}"])
            DMA("pool", wo_[:, 0:nh], wfo_d[j0:j0 + nh].rearrange("j p c -> p j c"), kq + "o", [], [kq + "o"])
            for c in range(4):
                sl = slice(c * 512, (c + 1) * 512)
                ab = actb[it % 2]
                abk = f"act{it % 2}"
                it += 1
                for jj in range(nh):
                    ba, bakey = nextbank()
                    for kc in range(8):
                        MM(ba[:, :], wa_[:, jj, kc, :], h2[:, kc, sl], kc == 0, kc == 7, [f"{kq}a{jj}", f"h2_{c}"], [bakey])
                    bu, bukey = nextbank()
                    for kc in range(8):
                        MM(bu[:, :], wu_[:, jj, kc, :], h2[:, kc, sl], kc == 0, kc == 7, [f"{kq}u{jj}", f"h2_{c}"], [bukey])
                    si = sit % 2
                    sit += 1
                    ACTF(saf[si], ba[:, :], AF.Silu, [bakey], [f"saf{si}"])
                    TT("dve", ab[:, jj, :], saf[si], bu[:, :], ALU.mult, [f"saf{si}", bukey], [abk])
                for ot in range(8):
                    by, bykey = nextbank()
                    for jj in range(nh):
                        MM(by[:, :], wo_[:, jj, ot * 128:(ot + 1) * 128], ab[:, jj, :], jj == 0, jj == nh - 1,
                           [kq + "o", abk], [bykey])
                    STT(x1[:, ot, sl], by[:, :], modT[:, 40 + ot:41 + ot], x1[:, ot, sl], ALU.mult, ALU.add,
                        [bykey, "modT", f"x1_{c}"], [f"x1_{c}"])
                if qi == 3:
                    DMA("sp", outT[:, :, sl], x1[:, :, sl], "out", [f"x1_{c}"], [])

        sems = {e: es.enter_context(nc.semaphore("s_" + e)) for e in Prog.ENGS}
        dsems = {k: es.enter_context(nc.semaphore("d_" + str(k))) for k in P.dma_keys}
        block = es.enter_context(nc.Block())
        P.emit(block, sems, dsems)
    return nc


_CACHE = {}


def _prep(inp):
    f = lambda a: np.asarray(a, dtype=np.float32)
    x = f(inp["x"])
    c = f(inp["c"])
    shared = {
        "cst": _constants(),
        "wada": _tile_w(f(inp["w_ada"])[0]),
        "win": _tile_w(f(inp["w_in"])[0]),
        "wpa": _tile_w(f(inp["w_proj_a"])[0]),
        "wpb": _tile_w(f(inp["w_proj_b"])[0]),
        "wo": _tile_w(f(inp["w_o"])[0]),
        "wfi": _tile_w(f(inp["w_ffn_in"])[0]),
        "wfo": np.ascontiguousarray(f(inp["w_ffn_out"])[0].reshape(22, 128, 1024)),
        "strips": _strips(f(inp["rpb"])[0]),
    }
    gq = np.stack([np.tile(f(inp[k])[0], 2) for k in ("g_qa", "g_ka", "g_qb", "g_kb")], axis=1)
    maps = []
    for core in range(8):
        b, q = core // 4, core % 4
        xe = np.zeros((4096, 1024), np.float32)
        lo = 2048 * q - 1024
        s0, s1 = max(lo, 0), min(lo + 4096, 8192)
        xe[s0 - lo:s1 - lo] = x[b, s0:s1]
        xTe = np.ascontiguousarray(xe.reshape(8, 512, 8, 128).transpose(0, 3, 2, 1))
        vec = np.zeros((128, NV), np.float32)
        vec[:, 0:8] = _col_vec(c[b])
        vec[:, 8:56] = _col_vec(f(inp["b_ada"])[0])
        vec[:, 56:64] = _col_vec(f(inp["g_norm1"])[0])
        vec[:, 64:72] = _col_vec(f(inp["g_norm2"])[0])
        vec[:, 72:88] = _col_vec(f(inp["b_gate"])[0])
        vec[:, 88:92] = gq
        vec[:, 92:181] = _vmask(q)
        C, S = _rope(q)
        m = dict(shared)
        m.update({"xT": xTe, "vecs": vec, "rm": _rowmask(q), "ropeC": C, "ropeS": S})
        maps.append(m)
    return maps


def kernel(**inputs):
    debug = bool(inputs.pop("_debug", False))
    key = ("nc", debug)
    if key not in _CACHE:
        _CACHE[key] = build_program(debug)
    nc = _CACHE[key]
    maps = _prep(inputs)
    res = run_bass_kernel_spmd(nc, maps, core_ids=list(range(8)))
    out = np.empty((2, 8192, 1024), np.float32)
    for core in range(8):
        b, q = core // 4, core % 4
        oT = res.results[core]["outT"]
        out[b, 2048 * q:2048 * (q + 1)] = oT.transpose(2, 1, 0).reshape(2048, 1024)
    if debug:
        kernel.last_results = res.results
    return out
```

```python
import contextlib
import numpy as np
import concourse.bass as bass
import concourse.mybir as mybir
from concourse.bass_utils import run_bass_kernel_spmd

F32 = mybir.dt.float32
BF16 = mybir.dt.bfloat16
AF = mybir.ActivationFunctionType
ALU = mybir.AluOpType
NEG = -30000.0
EPS = 1e-6
KB = 1024
DIL = (1, 4, 16)


class _Op:
    __slots__ = ("eng", "fn", "idx", "deps", "signal", "count", "dma_sem", "dma_val", "waits")

    def __init__(self, eng, fn, idx):
        self.eng = eng
        self.fn = fn
        self.idx = idx
        self.deps = set()
        self.signal = False
        self.count = 0
        self.dma_sem = None
        self.dma_val = 0
        self.waits = []


class Prog:
    ENGS = ("pe", "act", "dve", "pool", "sp")

    def __init__(self):
        self.ops = {e: [] for e in self.ENGS}
        self.writers = {}
        self.readers = {}
        self.dma_keys = {}
        self.last_dma = {}
        self.bar = set()

    def _add(self, eng, fn, reads, writes, semkey=None):
        o = _Op(eng, fn, len(self.ops[eng]))
        deps = set(self.bar)
        for r in reads:
            deps.update(self.writers.get(r, {}).values())
        for k in writes:
            deps.update(self.writers.get(k, {}).values())
            deps.update(self.readers.get(k, ()))
        deps.discard(o)
        o.deps = deps
        for r in reads:
            self.readers.setdefault(r, set()).add(o)
        wk = eng if semkey is None else ("dma", semkey)
        for k in writes:
            self.writers.setdefault(k, {})[wk] = o
            self.readers[k] = set()
        self.ops[eng].append(o)
        return o

    def op(self, eng, fn, reads=(), writes=()):
        return self._add(eng, fn, tuple(reads), tuple(writes))

    def dma(self, eng, fn, semkey, reads=(), writes=()):
        o = self._add(eng, fn, tuple(reads), tuple(writes), semkey)
        n = self.dma_keys.get(semkey, 0) + 1
        self.dma_keys[semkey] = n
        o.dma_sem = semkey
        o.dma_val = 16 * n
        self.last_dma[semkey] = o
        return o

    def alias(self, new_keys, old_keys):
        ws, rs = {}, set()
        for k in old_keys:
            for wk, o in self.writers.get(k, {}).items():
                cur = ws.get(wk)
                if cur is None or (o.dma_val if o.dma_sem is not None else o.idx) > \
                        (cur.dma_val if cur.dma_sem is not None else cur.idx):
                    ws[wk] = o
            rs |= self.readers.get(k, set())
        for nk in new_keys:
            d = self.writers.setdefault(nk, {})
            for wk, o in ws.items():
                cur = d.get(wk)
                if cur is None or (o.dma_val if o.dma_sem is not None else o.idx) > \
                        (cur.dma_val if cur.dma_sem is not None else cur.idx):
                    d[wk] = o
            self.readers.setdefault(nk, set()).update(rs)

    def barrier(self):
        b = set()
        for e in self.ENGS:
            for o in reversed(self.ops[e]):
                if o.dma_sem is None:
                    b.add(o)
                    break
        b.update(self.last_dma.values())
        self.bar = b
        self.writers = {}
        self.readers = {}

    def resolve(self):
        for X in self.ENGS:
            waited = {e: -1 for e in self.ENGS}
            waited_sem = {}
            for o in self.ops[X]:
                best = {}
                for d in o.deps:
                    if d.dma_sem is not None:
                        if waited_sem.get(d.dma_sem, 0) < d.dma_val:
                            k = ("s", d.dma_sem)
                            if k not in best or d.dma_val > best[k].dma_val:
                                best[k] = d
                    else:
                        if d.eng == X and X == "pe" and o.dma_sem is None:
                            continue
                        if d.idx > waited[d.eng]:
                            k = ("e", d.eng)
                            if k not in best or d.idx > best[k].idx:
                                best[k] = d
                for k, d in best.items():
                    if k[0] == "s":
                        waited_sem[d.dma_sem] = d.dma_val
                    else:
                        waited[d.eng] = d.idx
                        d.signal = True
                o.waits = list(best.values())
        for E in self.ENGS:
            c = 0
            for o in self.ops[E]:
                if o.dma_sem is None and o.signal:
                    c += 1
                    o.count = c

    def emit(self, block, sems, dsems):
        self.resolve()
        handles = {"pe": "tensor", "act": "scalar", "dve": "vector", "pool": "gpsimd", "sp": "sync"}
        prog = self

        def make(E):
            def body(eng):
                for o in prog.ops[E]:
                    for d in o.waits:
                        if d.dma_sem is not None:
                            eng.wait_ge(dsems[d.dma_sem], d.dma_val)
                        else:
                            eng.wait_ge(sems[d.eng], d.count)
                    ins = o.fn(eng)
                    if o.dma_sem is not None:
                        ins.then_inc(dsems[o.dma_sem], 16)
                    elif o.signal:
                        ins.then_inc(sems[E], 1)
                if E == "sp":
                    for k, n in prog.dma_keys.items():
                        eng.wait_ge(dsems[k], 16 * n)
            return body

        for E in self.ENGS:
            getattr(block, handles[E])(make(E))


def _tile_w(w):
    K, N = w.shape
    return np.ascontiguousarray(w.reshape(K // 128, 128, N // 128, 128).transpose(2, 1, 0, 3))


def _col_vec(v):
    return np.ascontiguousarray(v.reshape(-1, 128).T)


def _a_tiles(b):
    if b == 0:
        return list(range(0, 6))
    if b == 15:
        return list(range(14, 20))
    return list(range(b, b + 5))


EDGE_BLOCKS = (0, 1, 14, 15)
NV = 181


def _constants():
    c = np.zeros((128, 9 * 128), np.float32)
    idx = np.arange(128)
    c[:, 0:128] = np.eye(128, dtype=np.float32)
    c[:, 128:256] = 1.0
    c[:, 256:384] = (idx[:, None] // 64 == idx[None, :] // 64).astype(np.float32)
    r = np.zeros((128, 128), np.float32)
    for m in range(128):
        dd = m % 64
        if dd < 8:
            r[m + 8, m] = 1.0
        elif dd < 16:
            r[m - 8, m] = 1.0
    c[:, 384:512] = r
    k = idx[:, None]
    q = idx[None, :]
    mk0 = np.where(k >= q, 0.0, NEG)
    mk1 = np.where(k <= q, 0.0, NEG)
    c[:, 512:640] = mk0
    c[:, 640:768] = mk0
    c[:, 768:896] = mk1
    c[:, 896:1024] = mk1
    c[0, 1024:1088] = 1.0
    c[1, 1088:1152] = 1.0
    return c


def _strips(rpb):
    qc = np.arange(64)[None, :]
    kc = np.arange(64)[:, None]
    cs = np.clip(qc - 8, 0, 48)
    colmask = (kc >= cs) & (kc < cs + 16)
    coff = np.clip(kc - qc + 15, 0, 30)
    out = np.full((8, 2, 128, 16 * 64), NEG, np.float32)
    for ty in range(2):
        for half in range(2):
            for i in range(16):
                dr = 7 - i + half
                if abs(dr) > 7:
                    continue
                if ty == 0 and not (-4 <= dr <= 3):
                    continue
                g = rpb[:, dr + 7, :][:, coff]
                blk = np.where(colmask[None], g, np.float32(NEG))
                out[:, ty, half * 64:(half + 1) * 64, i * 64:(i + 1) * 64] = blk
    return out


def _rowmask(q):
    rm = np.full((2, 12 * 256), NEG, np.float32)
    for ge, G in enumerate((0, 7)):
        for s_ in range(6):
            t = 2 * G + s_
            ix = ge * 6 + s_
            for kl in range(2):
                kr = 32 * q + 2 * t - 4 + kl
                for j in range(4):
                    qr = 32 * q + 4 * G + j
                    rs = min(max(qr - 4, 0), 120)
                    if rs <= kr < rs + 8:
                        rm[kl, ix * 256 + j * 64: ix * 256 + (j + 1) * 64] = 0.0
    return rm


def _rope(q):
    pos = (2048 * q - 1024 + np.arange(4096)).astype(np.float32)
    inv = (np.float32(500000.0) ** (-(np.arange(8, dtype=np.float32) * np.float32(2.0)) / np.float32(16))).astype(np.float32)
    ang = (pos[:, None] * inv[None, :]).astype(np.float32).astype(np.float64)
    cs = np.cos(ang).astype(np.float32).T
    sn = np.sin(ang).astype(np.float32).T
    C = np.ones((128, 4096), np.float32)
    S = np.zeros((128, 4096), np.float32)
    for p in range(128):
        dd = p % 64
        if dd < 8:
            C[p] = cs[dd]
            S[p] = -sn[dd]
        elif dd < 16:
            C[p] = cs[dd - 8]
            S[p] = sn[dd - 8]
    return C, S


def _vmask(q):
    vm = np.ones((128, 89), np.float32)
    col = 20
    p = np.arange(128)
    for g, d in enumerate(DIL):
        lx = 2048 // d + 128
        nt = lx // 128
        for r in range(d):
            for i in range(nt):
                tok = 2048 * q - 64 * d + d * (128 * i + p) + r
                vm[:, col + r * nt + i] = ((tok >= 0) & (tok < 8192)).astype(np.float32)
        col += d * nt
    return vm


def build_program(debug=False):
    nc = bass.Bass("TRN2", target_bir_lowering=False)

    def din(name, shape):
        return nc.dram_tensor(name, shape, F32, kind="ExternalInput").ap()

    xT = din("xT", [8, 128, 8, 512])
    vecs_d = din("vecs", [128, NV])
    cst_d = din("cst", [128, 1152])
    wada_d = din("wada", [48, 128, 8, 128])
    win_d = din("win", [46, 128, 8, 128])
    wpa_d = din("wpa", [8, 128, 4, 128])
    wpb_d = din("wpb", [8, 128, 2, 128])
    wo_d = din("wo", [8, 128, 8, 128])
    wfi_d = din("wfi", [44, 128, 8, 128])
    wfo_d = din("wfo", [22, 128, 1024])
    strips_d = din("strips", [8, 2, 128, 1024])
    rm_d = din("rm", [2, 12 * 256])
    ropeC_d = din("ropeC", [128, 4096])
    ropeS_d = din("ropeS", [128, 4096])
    outT = nc.dram_tensor("outT", [128, 8, 2048], F32, kind="ExternalOutput").ap()
    dbg = {}
    if debug:
        dbg["hT"] = nc.dram_tensor("dbg_hT", [128, 8, 4096], F32, kind="ExternalOutput").ap()
        dbg["oaT"] = nc.dram_tensor("dbg_oaT", [128, 4, 2048], F32, kind="ExternalOutput").ap()
        dbg["obT"] = nc.dram_tensor("dbg_obT", [128, 2, 2048], F32, kind="ExternalOutput").ap()
        dbg["x1"] = nc.dram_tensor("dbg_x1", [128, 8, 2048], F32, kind="ExternalOutput").ap()
        dbg["mod"] = nc.dram_tensor("dbg_mod", [128, 48], F32, kind="ExternalOutput").ap()

    P = Prog()
    ARENA_KB = 192
    with contextlib.ExitStack() as es:
        arena = es.enter_context(nc.sbuf_tensor("arena", [128, ARENA_KB * KB // 2], BF16))
        cstb = es.enter_context(nc.sbuf_tensor("cstb", [128, 1152], BF16))
        vecs = es.enter_context(nc.sbuf_tensor("vecs_sb", [128, NV], F32))
        modT = es.enter_context(nc.sbuf_tensor("modT", [128, 48], F32))
        small = es.enter_context(nc.sbuf_tensor("small", [128, 64], F32))
        cactb_t = es.enter_context(nc.sbuf_tensor("cactb", [128, 16], BF16))
        pst = [es.enter_context(nc.psum_tensor(f"ps{i}", [128, 512], F32)) for i in range(8)]

        def av(off_b, nbytes, dt):
            a = arena[:, off_b // 2:(off_b + nbytes) // 2]
            return a if dt == BF16 else a.bitcast(F32)

        ident = cstb[:, 0:128]
        ones_b = cstb[:, 128:256]
        bones = cstb[:, 256:384]
        rmat = cstb[:, 384:512]
        mk2 = [cstb[:, 512:768], cstb[:, 768:1024]]
        oh2 = cstb[0:2, 1024:1152]

        V_C, V_BADA, V_G1, V_G2, V_BG, V_GQK, V_VM = 0, 8, 56, 64, 72, 88, 92
        gsc1 = small[:, 0:8]
        gsc2 = small[:, 8:16]
        gq = small[:, 16:20]
        cact2 = small[:, 24:40].rearrange("p (k n) -> p k n", n=2)
        stmp = small[:, 40:48]
        stmp2 = small[:, 48:56]

        bank_ctr = [0]
        bank_pool = [0, 1, 2, 3, 4, 5, 6]

        def nextbank():
            i = bank_pool[bank_ctr[0] % len(bank_pool)]
            bank_ctr[0] += 1
            return pst[i], f"ps{i}"

        def pipeline(items, stages):
            n, S = len(items), len(stages)
            st = [dict() for _ in items]
            for t in range(n + S - 1):
                for sidx in range(S):
                    i = t - sidx
                    if 0 <= i < n:
                        stages[sidx](items[i], st[i])

        def MM(out, lhsT, rhs, start, stop, r, w):
            P.op("pe", lambda e: e.matmul(out, lhsT=lhsT, rhs=rhs, start=start, stop=stop), r, w)

        def ACTF(out, in_, func, r, w, **kw):
            P.op("act", lambda e: e.activation(out=out, in_=in_, func=func, **kw), r, w)

        def TT(eng, out, in0, in1, op, r, w):
            P.op(eng, lambda e: e.tensor_tensor(out=out, in0=in0, in1=in1, op=op), r, w)

        def STT(out, in0, scalar, in1, op0, op1, r, w):
            P.op("dve", lambda e: e.scalar_tensor_tensor(out=out, in0=in0, scalar=scalar, in1=in1, op0=op0, op1=op1), r, w)

        def TS(eng, out, in0, s1, op0, r, w, s2=None, op1=None):
            if s2 is None:
                P.op(eng, lambda e: e.tensor_scalar(out=out, in0=in0, scalar1=s1, scalar2=None, op0=op0), r, w)
            else:
                P.op(eng, lambda e: e.tensor_scalar(out=out, in0=in0, scalar1=s1, scalar2=s2, op0=op0, op1=op1), r, w)

        def DMA(q, out, in_, key, r, w):
            P.dma(q, lambda e: e.dma_start(out=out, in_=in_), key, r, w)

        DMA("pool", cstb[:, :], cst_d[:, :], "cst", [], ["cst"])
        DMA("sp", vecs[:, :], vecs_d[:, :], "vecs", [], ["vecs"])
        cT = vecs[:, V_C:V_C + 8]
        ACTF(stmp, cT, AF.Exp, ["vecs"], ["stmp"], scale=-1.0)
        TS("dve", stmp, stmp, 1.0, ALU.add, ["stmp"], ["stmp"])
        P.op("dve", lambda e: e.reciprocal(out=stmp, in_=stmp), ["stmp"], ["stmp"])
        TT("dve", cact2[:, :, 0], cT, stmp, ALU.mult, ["vecs", "stmp"], ["cact"])
        TT("dve", cact2[:, :, 1], cT, stmp, ALU.mult, ["vecs", "stmp"], ["cact"])
        S0 = 88 * KB
        wada_ring = [av(64 * KB + i * 2 * KB, 2 * KB, BF16).rearrange("p (k c) -> p k c", k=8) for i in range(8)]
        cactb = cactb_t[:, :].rearrange("p (k n) -> p k n", n=2)
        P.op("dve", lambda e: e.tensor_copy(out=cactb, in_=cact2), ["cact"], ["cactb"])
        mbank, mkey = pst[7], "ps7"

        def mod_dma(ring, j0, j1):
            for j in range(j0, j1):
                DMA("pool", ring[j % 8], wada_d[j], f"wa{j % 8}", [], [f"wa{j % 8}"])

        def mod_mm(ring, bk, bkey, j0, j1):
            for j in range(j0, j1):
                for kc in range(8):
                    MM(bk[:, 2 * (j - j0):2 * (j - j0) + 2], ring[j % 8][:, kc, :], cactb[:, kc, :], kc == 0, kc == 7,
                       [f"wa{j % 8}", "cactb"], [bkey])
            mv = bk[:, 0:2 * (j1 - j0)].rearrange("p (j n) -> p j n", n=2)[:, :, 0]
            TT("dve", modT[:, j0:j1], mv, vecs[:, V_BADA + j0:V_BADA + j1], ALU.add, [bkey, "vecs"], ["modT"])

        mod_dma(wada_ring, 0, 8)
        mod_mm(wada_ring, mbank, mkey, 0, 8)
        mod_dma(wada_ring, 8, 16)
        mod_mm(wada_ring, mbank, mkey, 8, 16)
        TS("dve", stmp, modT[:, 8:16], 1.0, ALU.add, ["modT"], ["stmp"])
        TT("dve", gsc1, stmp, vecs[:, V_G1:V_G1 + 8], ALU.mult, ["stmp", "vecs"], ["gsc"])
        TS("dve", gq[:, :], vecs[:, V_GQK:V_GQK + 4], 1.0, ALU.mult, ["vecs"], ["gq"])
        TS("dve", gq[:, 0:1], vecs[:, V_GQK:V_GQK + 1], 0.125, ALU.mult, ["vecs", "gq"], ["gq"])
        TS("dve", gq[:, 2:3], vecs[:, V_GQK + 2:V_GQK + 3], 0.125, ALU.mult, ["vecs", "gq"], ["gq"])

        hT = av(0, 64 * KB, BF16).rearrange("p (k e) -> p k e", k=8)

        def hkeys(e0, n):
            return [f"hT{c}" for c in range(e0 // 512, (e0 + n - 1) // 512 + 1)]

        nrm_ctr = [0]

        def nrm_s0(it, st):
            i = nrm_ctr[0] % 2
            nrm_ctr[0] += 1
            st["i"] = i
            if it.get("pre") is not None:
                it["pre"]()
            sqb_ = it["bufs"]["sqb"][i]
            ACTF(sqb_, it["xb"], AF.Square, [it["xkey"]], [f"sqb{i}"])
            bk, bkey = it.get("bankfn", nextbank)()
            for kc in range(8):
                MM(bk[:, :], ones_b, sqb_[:, kc, :], kc == 0, kc == 7, [f"sqb{i}", "cst"], [bkey])
            st["bk"], st["bkey"] = bk, bkey

        def nrm_s1(it, st):
            i = st["i"]
            lnf_, tb = it["bufs"]["lnf"][i], it["bufs"]["tmpb"][i]
            tkey = it["bufs"].get("tkeys", ["tmpb0", "tmpb1"])[i]
            ACTF(lnf_, st["bk"][:, :], AF.Ln, [st["bkey"]], [f"lnf{i}"], scale=1.0 / 1024.0, bias=EPS)
            rsb, rskey = it.get("bankfn", nextbank)()
            ACTF(rsb[:, :], lnf_, AF.Exp, [f"lnf{i}"], [rskey], scale=-0.5)
            TT("dve", tb, it["xb"], rsb[:, :].unsqueeze(1).to_broadcast([128, 8, 512]), ALU.mult,
               [it["xkey"], rskey], [tkey])

        def nrm_s2(it, st):
            i = st["i"]
            tb = it["bufs"]["tmpb"][i]
            tkey = it["bufs"].get("tkeys", ["tmpb0", "tmpb1"])[i]
            gsc, shcol = it["gsc"], it["shcol"]
            for kc in range(8):
                sc_ap = gsc[:, kc:kc + 1]
                sh_ap = modT[:, shcol + kc:shcol + kc + 1]
                if kc % 2 == 0:
                    ACTF(it["dst_fn"](kc), tb[:, kc, :], AF.Identity, [tkey, "modT", "gsc"], it["dkeys"],
                         scale=sc_ap, bias=sh_ap)
                else:
                    TS("pool", it["dst_fn"](kc), tb[:, kc, :], sc_ap, ALU.mult, [tkey, "modT", "gsc"], it["dkeys"],
                       s2=sh_ap, op1=ALU.add)

        xbuf = [av(S0 + i * 16 * KB, 16 * KB, F32).rearrange("p (k n) -> p k n", k=8) for i in range(3)]
        nb1 = dict(sqb=[av(S0 + 48 * KB + i * 8 * KB, 8 * KB, BF16).rearrange("p (k n) -> p k n", k=8) for i in range(2)],
                   tmpb=[av(S0 + 64 * KB + i * 16 * KB, 16 * KB, F32).rearrange("p (k n) -> p k n", k=8) for i in range(2)],
                   lnf=[av(S0 + 96 * KB + i * 2 * KB, 2 * KB, F32) for i in range(2)])
        items = []
        for ec in range(8):
            def pre(ec=ec):
                DMA("sp", xbuf[ec % 3], xT[ec], f"xb{ec % 3}", [], [f"xb{ec % 3}"])
            items.append(dict(xb=xbuf[ec % 3], xkey=f"xb{ec % 3}", gsc=gsc1, shcol=0, pre=pre, bufs=nb1,
                              dst_fn=lambda kc, ec=ec: hT[:, kc, ec * 512:(ec + 1) * 512], dkeys=[f"hT{ec}"]))
        pipeline(items, [nrm_s0, nrm_s1, nrm_s2])
        bank_pool.append(7)
        if debug:
            dh = av(S0 + 64 * KB, 16 * KB, F32).rearrange("p (k n) -> p k n", k=8)
            for ec in range(8):
                P.op("dve", lambda e, ec=ec, dh=dh: e.tensor_copy(out=dh, in_=hT[:, :, ec * 512:(ec + 1) * 512]), [f"hT{ec}"], ["dh", "tmpb0"])
                DMA("sp", dbg["hT"][:, :, ec * 512:(ec + 1) * 512], dh, "dbg", ["dh"], [])
            P.barrier()
        P.alias(["oaT", "obT"], [f"wa{i}" for i in range(8)])
        P.alias(["jq", "jk"], ["xb0"])
        P.alias(["jv"], ["xb1"])
        P.alias(["stk"], ["xb1", "xb2"])
        P.alias(["rmb", "wr0", "wr1"], ["xb2"])
        P.alias(["wr2", "wr3", "wr4", "wr5"], ["sqb0"])
        P.alias(["pt0", "pt1"], ["sqb1", "tmpb0"])
        P.alias(["qsq0", "qsq1", "qln0", "qln1", "qrs0", "qrs1", "qqn0", "qqn1", "qqn2"], ["tmpb0", "tmpb1"])
        P.alias([f"wa{i}" for i in range(8)], ["tmpb1", "lnf0"])
        P.alias(["rcb0", "rcb1", "onesf"], ["lnf0", "lnf1"])

        oaT = av(64 * KB, 16 * KB, BF16).rearrange("p (k n) -> p k n", k=4)
        obT = av(80 * KB, 8 * KB, BF16).rearrange("p (k n) -> p k n", k=2)
        o = S0
        jq2 = av(o, 8 * KB, BF16).rearrange("p (h n) -> p h n", h=2); o += 8 * KB
        jk = av(o, 8 * KB, BF16); o += 8 * KB
        jv = av(o, 12 * KB, BF16).rearrange("p (t c) -> p t c", c=192); o += 12 * KB
        acc = [av(o + i * 8 * KB, 8 * KB, F32) for i in range(2)]
        stripk = av(o, 8 * KB, BF16).rearrange("p (h t c) -> p h t c", h=2, t=2)
        rmb = av(o + 8 * KB, 6 * KB, BF16)
        o += 16 * KB
        wring = [av(o + i * 2 * KB, 2 * KB, BF16).rearrange("p (k c) -> p k c", k=8) for i in range(6)]; o += 12 * KB
        ptb = [av(o + i * 6 * KB, 6 * KB, BF16) for i in range(2)]; o += 12 * KB
        q_sq = [av(o + i * KB, KB, BF16) for i in range(2)]; o += 2 * KB
        q_ln = [av(o + i * 2 * KB, 2 * KB, F32) for i in range(2)]; o += 4 * KB
        q_rs = [av(o + i * 2 * KB, 2 * KB, F32) for i in range(2)]; o += 4 * KB
        q_qn = [av(o + i * KB, KB, BF16) for i in range(3)]; o += 3 * KB
        q_t1_off = o
        q_t1 = [av(o + i * 2 * KB, 2 * KB, F32) for i in range(2)]; o += 4 * KB
        q_t2 = [av(o + i * 2 * KB, 2 * KB, F32) for i in range(2)]; o += 4 * KB
        ropeCb = [av(o + i * 2 * KB, 2 * KB, F32) for i in range(2)]; o += 4 * KB
        ropeSb = [av(o + i * 2 * KB, 2 * KB, F32) for i in range(2)]; o += 4 * KB
        rcb = [av(o + i * 2 * KB, 2 * KB, F32) for i in range(2)]; o += 4 * KB
        onesf = av(o, 256, F32); o += 256
        assert o <= ARENA_KB * KB, o

        wada_ring2 = [av(q_t1_off + i * 2 * KB, 2 * KB, BF16).rearrange("p (k c) -> p k c", k=8) for i in range(8)]
        P.op("pool", lambda e: e.memset(onesf, 1.0), [], ["onesf"])
        P.op("pool", lambda e: e.memset(jq2, 0.0), [], ["jq"])
        DMA("pool", rmb[0:2, :], rm_d[:, :], "rmb", [], ["rmb"])

        wr_ctr = [0]

        wr_pref = ["wr"]

        def load_w(src, kcn=8):
            i = wr_ctr[0] % len(wring)
            wr_ctr[0] += 1
            dst = wring[i] if kcn == 8 else wring[i][:, 0:kcn, :]
            k = f"{wr_pref[0]}{i}"
            DMA("pool", dst, src, k, [], [k])
            return wring[i], k

        def proj_fm(w, wkey, e0, n):
            bk, bkey = nextbank()
            for kc in range(8):
                MM(bk[:, :n], w[:, kc, :], hT[:, kc, e0:e0 + n], kc == 0, kc == 7, [wkey] + hkeys(e0, n), [bkey])
            return bk, bkey

        qk_ctr = [0]

        def qk_s0(it, st):
            i = qk_ctr[0] % 2
            st["i"] = i
            st["j"] = qk_ctr[0] % 3
            qk_ctr[0] += 1
            n = it["n"]
            st["bk"], st["bkey"] = proj_fm(it["w"], it["wkey"], it["e0"], n)
            ACTF(q_sq[i][:, :n], st["bk"][:, :n], AF.Square, [st["bkey"]], [f"qsq{i}"])

        def qk_s1(it, st):
            i, j, n, bk, bkey = st["i"], st["j"], it["n"], st["bk"], st["bkey"]
            b2, b2key = nextbank()
            MM(b2[:, :n], bones, q_sq[i][:, :n], True, True, [f"qsq{i}", "cst"], [b2key])
            ACTF(q_ln[i][:, :n], b2[:, :n], AF.Ln, [b2key], [f"qln{i}"], scale=1.0 / 64.0, bias=EPS)
            ACTF(q_rs[i][:, :n], q_ln[i][:, :n], AF.Exp, [f"qln{i}"], [f"qrs{i}"], scale=-0.5)
            gc = it["gcol"]
            if it["rot"]:
                STT(q_qn[j][:, :n], bk[:, :n], gq[:, gc:gc + 1], q_rs[i][:, :n], ALU.mult, ALU.mult,
                    [bkey, f"qrs{i}", "gq"], [f"qqn{j}"])
            else:
                for ps, oap in it["outs"]:
                    STT(oap, bk[ps, :n], gq[ps, gc:gc + 1], q_rs[i][ps, :n], ALU.mult, ALU.mult,
                        [bkey, f"qrs{i}", "gq"], it["okeys"])

        def qk_s2(it, st):
            if not it["rot"]:
                return
            i, n = st["i"], it["n"]
            DMA("sp", ropeCb[i][:, :n], ropeC_d[:, it["e0"]:it["e0"] + n], f"rc{i}", [], [f"rc{i}"])
            DMA("sp", ropeSb[i][:, :n], ropeS_d[:, it["e0"]:it["e0"] + n], f"rs{i}", [], [f"rs{i}"])

        def qk_s3(it, st):
            if not it["rot"]:
                return
            i, j, n, d = st["i"], st["j"], it["n"], it["d"]
            b3, b3key = nextbank()
            MM(b3[:, :n], rmat, q_qn[j][:, :n], True, True, [f"qqn{j}", "cst"], [b3key])
            TT("pool", q_t1[i][:, :n], q_qn[j][:, :n], ropeCb[i][:, :n], ALU.mult, [f"qqn{j}", f"rc{i}"], [f"qt1{i}"])
            TT("dve", q_t2[i][:, :n], b3[:, :n], ropeSb[i][:, :n], ALU.mult, [b3key, f"rs{i}"], [f"qt2{i}"])
            for hi, (ps, oap) in enumerate(it["outs"]):
                TT("dve", oap, q_t1[i][ps, :n].rearrange("p (a r) -> p a r", r=d),
                   q_t2[i][ps, :n].rearrange("p (a r) -> p a r", r=d), ALU.add, [f"qt1{i}", f"qt2{i}"], it["okeys"])

        def vtile(wv, wvkey, lhs_fn, hk, dst, vmcol):
            bk, bkey = nextbank()
            for kc in range(8):
                MM(bk[:, :128], lhs_fn(kc), wv[:, kc, :], kc == 0, kc == 7, [wvkey] + hk, [bkey])
            vm = vecs[:, V_VM + vmcol:V_VM + vmcol + 1]
            ACTF(dst.rearrange("p (b c) -> p b c", c=64)[:, 0:3:2, :], bk[:, :128].rearrange("p (b c) -> p b c", c=64),
                 AF.Copy, [bkey, "vecs"], ["jv"], scale=vm)

        def vones(vmbase, ntiles):
            src = vecs[:, V_VM + vmbase:V_VM + vmbase + ntiles].unsqueeze(2).to_broadcast([128, ntiles, 64])
            P.op("dve", lambda e: e.tensor_copy(out=jv[:, 0:ntiles, 64:128], in_=src), ["vecs"], ["jv"])

        at_ctr = [0]

        def at_s0(it, st):
            slots, pref = it.get("pts", (ptb, "pt"))
            pi = at_ctr[0] % len(slots)
            at_ctr[0] += 1
            st["pt"], st["ptk"] = slots[pi], f"{pref}{pi}"
            pt, ptk = st["pt"], st["ptk"]
            nq = it["nq"]
            per_bank = 512 // (2 * nq)
            banks = []
            for i, (kc0, vt) in enumerate(it["ktiles"]):
                if i % per_bank == 0:
                    banks.append(nextbank())
                bk, bkey = banks[-1]
                c0 = (i % per_bank) * 2 * nq
                dst = bk[:, c0:c0 + 2 * nq]
                sd = it.get("step", 1)
                q0 = it["qcols"]
                MM(dst.rearrange("p (h n) -> p h n", h=2), jk[:, kc0:kc0 + sd * 127 + 1:sd],
                   jq2[:, :, q0:q0 + sd * (nq - 1) + 1:sd], True, False, ["jk", "jq"], [bkey])
                it["extra"](i, dst, bkey)
            nt = len(it["ktiles"])
            for bi, (bk, bkey) in enumerate(banks):
                ntb = min(per_bank, nt - per_bank * bi)
                ACTF(pt[:, bi * 512:bi * 512 + ntb * 2 * nq], bk[:, :ntb * 2 * nq], AF.Exp, [bkey], [ptk])

        def at_noop(it, st):
            pass

        def at_s1(it, st):
            pt, ptk = st["pt"], st["ptk"]
            nq = it["nq"]
            kts = it["ktiles"]
            nt = len(kts)
            bo, bokey = nextbank()
            for h in range(2):
                for i, (kc0, vt) in enumerate(kts):
                    lhs = jv[:, vt, 0:128] if h == 0 else jv[:, vt, 64:192]
                    c0 = i * 2 * nq + h * nq
                    MM(bo[:, h * nq:(h + 1) * nq], lhs, pt[:, c0:c0 + nq], i == 0, i == nt - 1, ["jv", ptk], [bokey])
            it["pvdst"](bo, bokey)

        nctr = [0]

        def normalize_to(bo_ap_fn, bokey, hl, n, dst, dkeys):
            i = nctr[0] % 2
            nctr[0] += 1
            rc = rcb[i]
            nu = slice(0, 64) if hl == 0 else slice(64, 128)
            de = slice(64, 128) if hl == 0 else slice(0, 64)
            P.op("dve", lambda e: e.reciprocal(out=rc[nu, 0:n], in_=bo_ap_fn(de)), [bokey], [f"rcb{i}"])
            TT("dve", dst, bo_ap_fn(nu), rc[nu, 0:n], ALU.mult, [bokey, f"rcb{i}"], dkeys)

        for hp in range(4):
            wq, wqk = load_w(win_d[hp])
            wk, wkk = load_w(win_d[4 + hp])
            wv, wvk = load_w(win_d[8 + hp])
            DMA("pool", stripk, strips_d[2 * hp:2 * hp + 2].rearrange("h t p c -> p h t c"), "stk", [], ["stk"])
            mod_dma(wada_ring2, 16 + 8 * hp, 24 + 8 * hp)
            items = []
            for c in range(4):
                sl = slice(c * 512, (c + 1) * 512)
                items.append(dict(w=wq, wkey=wqk, e0=1024 + 512 * c, n=512, gcol=0, rot=False, okeys=["jq"],
                                  outs=[(slice(0, 64), jq2[0:64, 0, sl]), (slice(64, 128), jq2[64:128, 1, sl])]))
            for (e0, n) in [(768, 256)] + [(1024 + 512 * c, 512) for c in range(4)] + [(3072, 256)]:
                items.append(dict(w=wk, wkey=wkk, e0=e0, n=n, gcol=1, rot=False, okeys=["jk"],
                                  outs=[(slice(0, 128), jk[:, e0 - 768:e0 - 768 + n])]))
            pipeline(items, [qk_s0, qk_s1])
            mb_, mbk_ = nextbank()
            mod_mm(wada_ring2, mb_, mbk_, 16 + 8 * hp, 24 + 8 * hp)
            if hp == 3:
                TS("dve", stmp2, modT[:, 32:40], 1.0, ALU.add, ["modT"], ["stmp2"])
                TT("dve", gsc2, stmp2, vecs[:, V_G2:V_G2 + 8], ALU.mult, ["stmp2", "vecs"], ["gsc"])
                if debug:
                    DMA("sp", dbg["mod"][:, :], modT[:, :], "dbg", ["modT"], [])
            vones(0, 20)
            for t in range(20):
                e0 = 768 + 128 * t
                vtile(wv, wvk, lambda kc, e0=e0: hT[:, kc, e0:e0 + 128], hkeys(e0, 128), jv[:, t, :], t)
            items = []
            for G in range(8):
                edge = G in (0, 7)

                def extra(i, dst, bkey, G=G, edge=edge):
                    i0 = 11 - 2 * i
                    ty = 1 if edge else 0
                    MM(dst.rearrange("p (h n) -> p h n", h=2), ident, stripk[:, :, ty, i0 * 64:i0 * 64 + 256],
                       False, not edge, ["stk", "cst"], [bkey])
                    if edge:
                        ix = (0 if G == 0 else 1) * 6 + i
                        for h in range(2):
                            MM(dst[:, h * 256:(h + 1) * 256], oh2, rmb[0:2, ix * 256:(ix + 1) * 256], False, h == 1,
                               ["rmb", "cst"], [bkey])

                def pvdst(bo, bokey, G=G, hp=hp):
                    for h in range(2):
                        normalize_to(lambda ps, h=h: bo[ps, h * 256:(h + 1) * 256], bokey, h, 256,
                                     oaT[64 * h:64 * h + 64, hp, G * 256:(G + 1) * 256], ["oaT"])

                items.append(dict(nq=256, qcols=G * 256, ktiles=[((2 * G + s_) * 128, 2 * G + s_) for s_ in range(6)],
                                  extra=extra, pvdst=pvdst))
            pipeline(items, [at_s0, at_s1])
        ptB = ([ptb[0][:, i * 512:(i + 1) * 512] for i in range(3)], "pB")
        P.alias(["pB0", "pB1", "pB2"], ["pt0"])
        P.alias(["acc0", "acc1"], ["stk", "rmb"])
        P.alias(["qt10", "qt11", "qt20", "qt21", "rc0", "rc1", "rs0", "rs1"], [f"wa{i}" for i in range(8)])

        pending = []
        for hp in range(2):
            for gi, g in enumerate((0, 1, 2) if hp == 0 else (2, 0, 1)):
                d = DIL[g]
                L = 2048 // d
                Lx = L + 128
                ntl = Lx // 128
                ebase = 1024 - 64 * d
                vmbase = 20 + sum(DIL[gg] * ((2048 // DIL[gg] + 128) // 128) for gg in range(g))
                wq, wqk = load_w(win_d[12 + 2 * g + hp])
                wk, wkk = load_w(win_d[18 + 2 * g + hp])
                wv, wvk = load_w(win_d[24 + 2 * g + hp])
                items = []
                for c in range(4):
                    items.append(dict(w=wq, wkey=wqk, e0=1024 + 512 * c, n=512, gcol=2, rot=True, d=1, okeys=["jq"],
                                      outs=[(slice(64 * h, 64 * h + 64),
                                             jq2[64 * h:64 * h + 64, h, c * 512:(c + 1) * 512].unsqueeze(2))
                                            for h in range(2)]))
                tp = 0
                while tp < 2048 + 128 * d:
                    n = min(512, 2048 + 128 * d - tp)
                    items.append(dict(w=wk, wkey=wkk, e0=ebase + tp, n=n, gcol=3, rot=True, d=1, okeys=["jk"],
                                      outs=[(slice(0, 128), jk[:, tp:tp + n].unsqueeze(2))]))
                    tp += n
                pipeline(items, [qk_s0, qk_s1, qk_s2, qk_s3])
                vones(vmbase, d * ntl)
                for r in range(d):
                    for it_ in range(ntl):
                        es_ = ebase + r + 128 * d * it_
                        tix = r * ntl + it_
                        vtile(wv, wvk, lambda kc, es_=es_, d=d: hT[:, kc, es_:es_ + 127 * d + 1:d],
                              hkeys(es_, 127 * d + 1), jv[:, tix, :], vmbase + tix)
                        if pending and tix % 2 == 1:
                            pending.pop(0)()
                while pending:
                    pending.pop(0)()
                items = []
                for r in range(d):
                    for bb in range(L // 128):
                        def extra(i, dst, bkey):
                            MM(dst, ident, mk2[i], False, True, ["cst"], [bkey])

                        def pvdst(bo, bokey, r=r, bb=bb, d=d, gi=gi):
                            for h in range(2):
                                a3 = acc[h].rearrange("p (a r) -> p a r", r=d)[:, 128 * bb:128 * (bb + 1), r]
                                src = bo[:, h * 128:(h + 1) * 128]
                                if gi == 0:
                                    ACTF(a3, src, AF.Copy, [bokey], [f"acc{h}"])
                                else:
                                    TT("dve", a3, src, a3, ALU.add, [bokey, f"acc{h}"], [f"acc{h}"])

                        kts = [(r + d * 128 * (bb + kt), r * ntl + bb + kt) for kt in range(2)]
                        items.append(dict(nq=128, qcols=r + d * 128 * bb, ktiles=kts, extra=extra, pvdst=pvdst, step=d,
                                          pts=ptB))
                pipeline(items, [at_s0, at_noop, at_s1])
            def merge_piece(hl, c, hp=hp):
                i = nctr[0] % 2
                nctr[0] += 1
                rc = rcb[i]
                sl = slice(c * 512, (c + 1) * 512)
                nu = slice(0, 64) if hl == 0 else slice(64, 128)
                de = slice(64, 128) if hl == 0 else slice(0, 64)
                P.op("dve", lambda e: e.reciprocal(out=rc[nu, :], in_=acc[hl][de, sl]), [f"acc{hl}"], [f"rcb{i}"])
                TT("dve", obT[nu, hp, sl], acc[hl][nu, sl], rc[nu, :], ALU.mult, [f"acc{hl}", f"rcb{i}"], ["obT"])

            for c in range(4):
                for hl in range(2):
                    pending.append(lambda hl=hl, c=c, mp=merge_piece: mp(hl, c))
        if debug:
            while pending:
                pending.pop(0)()
            dh = av(S0, 8 * KB, F32)
            for k in range(4):
                P.op("dve", lambda e, k=k, dh=dh: e.tensor_copy(out=dh, in_=oaT[:, k, :]), ["oaT", "jq"], ["dh", "jq"])
                DMA("sp", dbg["oaT"][:, k, :], dh, "dbg", ["dh"], [])
            for k in range(2):
                P.op("dve", lambda e, k=k, dh=dh: e.tensor_copy(out=dh, in_=obT[:, k, :]), ["obT", "jq"], ["dh"])
                DMA("sp", dbg["obT"][:, k, :], dh, "dbg", ["dh"], [])
            P.barrier()

        mg = av(S0, 32 * KB, BF16).rearrange("p (k n) -> p k n", k=8)
        wring = [av(160 * KB + i * 2 * KB, 2 * KB, BF16).rearrange("p (k c) -> p k c", k=8) for i in range(8)]
        wr_pref[0] = "we"
        P.alias([f"mg{c}" for c in range(4)], ["jq", "jk", "jv", "acc0"])
        P.alias([f"we{i}" for i in range(8)],
                ["qln1", "qrs0", "qrs1", "qqn0", "qqn1", "qqn2", "qt10", "qt11", "qt20", "qt21"])
        P.alias(["gaf0", "gaf1", "gbf0", "gbf1"], ["wr2", "wr3", "wr4", "wr5"])
        P.alias(["t1f0", "t1f1", "t2f0", "t2f1"], ["pt0", "pt1", "pB0", "pB1", "pB2"])
        P.alias([f"xr{i}" for i in range(4)], ["pt1", "qsq0", "qsq1", "qln0"])
        o = S0 + 48 * KB
        gaf = [av(o + i * 2 * KB, 2 * KB, F32) for i in range(2)]; o += 4 * KB
        gbf = [av(o + i * 2 * KB, 2 * KB, F32) for i in range(2)]; o += 4 * KB
        t1f = [av(o + i * 2 * KB, 2 * KB, F32) for i in range(2)]; o += 4 * KB
        t2f = [av(o + i * 2 * KB, 2 * KB, F32) for i in range(2)]; o += 4 * KB
        xres = [av(o + i * 2 * KB, 2 * KB, F32) for i in range(4)]; o += 8 * KB
        it = 0
        def ep1_loads(ot):
            return (load_w(win_d[30 + ot]), load_w(win_d[38 + ot]), load_w(wpa_d[ot], 4), load_w(wpb_d[ot], 2))

        nxt = ep1_loads(0)
        for ot in range(8):
            (wga, wgak), (wgb, wgbk), (wpa, wpak), (wpb, wpbk) = nxt
            if ot < 7:
                nxt = ep1_loads(ot + 1)
            for c in range(4):
                i = it % 2
                it += 1
                e0 = 1024 + 512 * c
                sl = slice(c * 512, (c + 1) * 512)
                if ot == 0:
                    for _ in range(2):
                        if pending:
                            pending.pop(0)()
                bk, bkey = proj_fm(wga, wgak, e0, 512)
                ACTF(gaf[i], bk[:, :], AF.Sigmoid, [bkey, "vecs"], [f"gaf{i}"], bias=vecs[:, V_BG + ot:V_BG + ot + 1])
                bk, bkey = proj_fm(wgb, wgbk, e0, 512)
                ACTF(gbf[i], bk[:, :], AF.Sigmoid, [bkey, "vecs"], [f"gbf{i}"], bias=vecs[:, V_BG + 8 + ot:V_BG + 9 + ot])
                bk, bkey = nextbank()
                for kc in range(4):
                    MM(bk[:, :], wpa[:, kc, :], oaT[:, kc, sl], kc == 0, kc == 3, [wpak, "oaT"], [bkey])
                TT("dve", t1f[i], bk[:, :], gaf[i], ALU.mult, [bkey, f"gaf{i}"], [f"t1f{i}"])
                bk, bkey = nextbank()
                for kc in range(2):
                    MM(bk[:, :], wpb[:, kc, :], obT[:, kc, sl], kc == 0, kc == 1, [wpbk, "obT"], [bkey])
                TT("dve", t2f[i], bk[:, :], gbf[i], ALU.mult, [bkey, f"gbf{i}"], [f"t2f{i}"])
                TT("pool", mg[:, ot, sl], t1f[i], t2f[i], ALU.add, [f"t1f{i}", f"t2f{i}"], [f"mg{c}"])
        P.barrier()

        x1 = av(0, 64 * KB, F32).rearrange("p (k n) -> p k n", k=8)
        h2 = av(160 * KB, 32 * KB, BF16).rearrange("p (k n) -> p k n", k=8)
        tb2 = av(64 * KB, 16 * KB, F32).rearrange("p (k n) -> p k n", k=8)
        nb2 = dict(sqb=[av(136 * KB + i * 8 * KB, 8 * KB, BF16).rearrange("p (k n) -> p k n", k=8) for i in range(2)],
                   tmpb=[tb2, tb2], tkeys=["tmpb0", "tmpb0"],
                   lnf=[av(80 * KB + i * 2 * KB, 2 * KB, F32) for i in range(2)])
        wring = [av(120 * KB + i * 2 * KB, 2 * KB, BF16).rearrange("p (k c) -> p k c", k=8) for i in range(8)]
        wr_pref[0] = "w2_"
        wos = [load_w(wo_d[ot]) for ot in range(8)]
        del bank_pool[:]
        bank_pool.extend([0, 1, 2, 3, 4])
        n2_bank_ctr = [0]

        def n2_bank():
            i = 5 + n2_bank_ctr[0] % 3
            n2_bank_ctr[0] += 1
            return pst[i], f"ps{i}"

        n2_items = []
        for c in range(4):
            n2_items.append(dict(xb=x1[:, :, c * 512:(c + 1) * 512], xkey=f"x1_{c}", gsc=gsc2, shcol=24, bufs=nb2,
                                 bankfn=n2_bank,
                                 dst_fn=lambda kc, c=c: h2[:, kc, c * 512:(c + 1) * 512], dkeys=[f"h2_{c}"]))
        n2_st = [dict() for _ in range(4)]
        n2_stages = [nrm_s0, nrm_s1, nrm_s2]
        it = 0
        for c in range(5):
            sl = slice(c * 512, (c + 1) * 512) if c < 4 else None
            for ot in range(8):
                if c < 4:
                    wo, wok = wos[ot]
                    i = it % 4
                    it += 1
                    DMA("sp", xres[i], xT[2 + c][:, ot, :], f"xr{i}", [], [f"xr{i}"])
                    bk, bkey = nextbank()
                    for kc in range(8):
                        MM(bk[:, :], wo[:, kc, :], mg[:, kc, sl], kc == 0, kc == 7, [wok, f"mg{c}"], [bkey])
                    STT(x1[:, ot, sl], bk[:, :], modT[:, 16 + ot:17 + ot], xres[i], ALU.mult, ALU.add,
                        [bkey, "modT", f"xr{i}"], [f"x1_{c}"])
                if c >= 1:
                    if ot == 2:
                        nrm_s0(n2_items[c - 1], n2_st[c - 1])
                    elif ot == 5:
                        nrm_s1(n2_items[c - 1], n2_st[c - 1])
                    elif ot == 7:
                        nrm_s2(n2_items[c - 1], n2_st[c - 1])
        if debug:
            for c in range(4):
                DMA("sp", dbg["x1"][:, :, c * 512:(c + 1) * 512], x1[:, :, c * 512:(c + 1) * 512], "dbg", [f"x1_{c}"], [])
        del bank_pool[:]
        bank_pool.extend(range(8))
        ep_mg = [f"mg{c}" for c in range(4)]
        P.alias(["fq0a", "fq0u", "fq0o"] + [f"fq0a{j}" for j in range(6)] + [f"fq0u{j}" for j in range(6)],
                ep_mg + [f"w2_{i}" for i in range(8)])
        P.alias(["fq1a", "fq1u", "fq1o"] + [f"fq1a{j}" for j in range(6)] + [f"fq1u{j}" for j in range(6)],
                ["tmpb0", "lnf0", "lnf1"] + ep_mg)
        P.alias(["act0", "act1", "saf0", "saf1"], ["sqb0", "sqb1"])

        quarters = [list(range(0, 6)), list(range(6, 12)), list(range(12, 17)), list(range(17, 22))]
        wqb = []
        for qi in range(2):
            base = (100 if qi == 0 else 64) * KB
            wa_ = av(base, 12 * KB, BF16).rearrange("p (j k c) -> p j k c", j=6, k=8)
            wu_ = av(base + 12 * KB, 12 * KB, BF16).rearrange("p (j k c) -> p j k c", j=6, k=8)
            wo_ = av(base + 24 * KB, 12 * KB, BF16).rearrange("p (j c) -> p j c", j=6)
            wqb.append((wa_, wu_, wo_))
        actb = [av(136 * KB + i * 6 * KB, 6 * KB, BF16).rearrange("p (j n) -> p j n", j=6) for i in range(2)]
        saf = [av(148 * KB + i * 2 * KB, 2 * KB, F32) for i in range(2)]
        it = 0
        sit = 0
        for qi, hid in enumerate(quarters):
            wa_, wu_, wo_ = wqb[qi % 2]
            nh = len(hid)
            j0 = hid[0]
            kq = f"fq{qi % 2}"
            for jj in range(nh):
                DMA("pool", wa_[:, jj], wfi_d[j0 + jj], f"{kq}a{jj}", [], [f"{kq}a{jj}"])
                DMA("pool", wu_[:, jj], wfi_d[22 + j0 + jj], f"{kq}u{jj}", [], [f"{kq}u{jj}"])
            DMA("pool", wo_[:, 0:nh], wfo_d[j0:j0 + nh].rearrange("j p c -> p j c"), kq + "o", [], [kq + "o"])
            for c in range(4):
                sl = slice(c * 512, (c + 1) * 512)
                ab = actb[it % 2]
                abk = f"act{it % 2}"
                it += 1
                for jj in range(nh):
                    ba, bakey = nextbank()
                    for kc in range(8):
                        MM(ba[:, :], wa_[:, jj, kc, :], h2[:, kc, sl], kc == 0, kc == 7, [f"{kq}a{jj}", f"h2_{c}"], [bakey])
                    bu, bukey = nextbank()
                    for kc in range(8):
                        MM(bu[:, :], wu_[:, jj, kc, :], h2[:, kc, sl], kc == 0, kc == 7, [f"{kq}u{jj}", f"h2_{c}"], [bukey])
                    si = sit % 2
                    sit += 1
                    ACTF(saf[si], ba[:, :], AF.Silu, [bakey], [f"saf{si}"])
                    TT("dve", ab[:, jj, :], saf[si], bu[:, :], ALU.mult, [f"saf{si}", bukey], [abk])
                for ot in range(8):
                    by, bykey = nextbank()
                    for jj in range(nh):
                        MM(by[:, :], wo_[:, jj, ot * 128:(ot + 1) * 128], ab[:, jj, :], jj == 0, jj == nh - 1,
                           [kq + "o", abk], [bykey])
                    STT(x1[:, ot, sl], by[:, :], modT[:, 40 + ot:41 + ot], x1[:, ot, sl], ALU.mult, ALU.add,
                        [bykey, "modT", f"x1_{c}"], [f"x1_{c}"])
                if qi == 3:
                    DMA("sp", outT[:, :, sl], x1[:, :, sl], "out", [f"x1_{c}"], [])

        sems = {e: es.enter_context(nc.semaphore("s_" + e)) for e in Prog.ENGS}
        dsems = {k: es.enter_context(nc.semaphore("d_" + str(k))) for k in P.dma_keys}
        block = es.enter_context(nc.Block())
        P.emit(block, sems, dsems)
    return nc


_CACHE = {}


def _prep(inp):
    f = lambda a: np.asarray(a, dtype=np.float32)
    x = f(inp["x"])
    c = f(inp["c"])
    shared = {
        "cst": _constants(),
        "wada": _tile_w(f(inp["w_ada"])[0]),
        "win": _tile_w(f(inp["w_in"])[0]),
        "wpa": _tile_w(f(inp["w_proj_a"])[0]),
        "wpb": _tile_w(f(inp["w_proj_b"])[0]),
        "wo": _tile_w(f(inp["w_o"])[0]),
        "wfi": _tile_w(f(inp["w_ffn_in"])[0]),
        "wfo": np.ascontiguousarray(f(inp["w_ffn_out"])[0].reshape(22, 128, 1024)),
        "strips": _strips(f(inp["rpb"])[0]),
    }
    gq = np.stack([np.tile(f(inp[k])[0], 2) for k in ("g_qa", "g_ka", "g_qb", "g_kb")], axis=1)
    maps = []
    for core in range(8):
        b, q = core // 4, core % 4
        xe = np.zeros((4096, 1024), np.float32)
        lo = 2048 * q - 1024
        s0, s1 = max(lo, 0), min(lo + 4096, 8192)
        xe[s0 - lo:s1 - lo] = x[b, s0:s1]
        xTe = np.ascontiguousarray(xe.reshape(8, 512, 8, 128).transpose(0, 3, 2, 1))
        vec = np.zeros((128, NV), np.float32)
        vec[:, 0:8] = _col_vec(c[b])
        vec[:, 8:56] = _col_vec(f(inp["b_ada"])[0])
        vec[:, 56:64] = _col_vec(f(inp["g_norm1"])[0])
        vec[:, 64:72] = _col_vec(f(inp["g_norm2"])[0])
        vec[:, 72:88] = _col_vec(f(inp["b_gate"])[0])
        vec[:, 88:92] = gq
        vec[:, 92:181] = _vmask(q)
        C, S = _rope(q)
        m = dict(shared)
        m.update({"xT": xTe, "vecs": vec, "rm": _rowmask(q), "ropeC": C, "ropeS": S})
        maps.append(m)
    return maps


def kernel(**inputs):
    debug = bool(inputs.pop("_debug", False))
    key = ("nc", debug)
    if key not in _CACHE:
        _CACHE[key] = build_program(debug)
    nc = _CACHE[key]
    maps = _prep(inputs)
    res = run_bass_kernel_spmd(nc, maps, core_ids=list(range(8)))
    out = np.empty((2, 8192, 1024), np.float32)
    for core in range(8):
        b, q = core // 4, core % 4
        oT = res.results[core]["outT"]
        out[b, 2048 * q:2048 * (q + 1)] = oT.transpose(2, 1, 0).reshape(2048, 1024)
    if debug:
        kernel.last_results = res.results
    return out
```
